# Optimizing a Trainium2 kernel written in Bass

```python
import jax, jax.numpy as jnp
from jax import lax
import numpy as np

D_MODEL = 1024
BATCH = 8
SEQ = 4096
DEPTH = 1
DEC_BATCH = 32
DEC_SEQ = 64
PAST_LEN = 1024

CHUNK = 64
D_MIX = 2 * D_MODEL
D_A = D_MIX // 2
GMLP_HEADS = 8
GMLP_HEAD_DIM = D_A // GMLP_HEADS
GMLP_CHUNK = 128
D_B = D_MIX - D_A
SSM_HEAD_DIM = 64
SSM_HEADS = D_B // SSM_HEAD_DIM
SSM_GROUPS = 2
SSM_STATE = 128
GN = SSM_GROUPS * SSM_STATE
CONV_W = 4
CONV_DIM = D_B + 2 * GN
SSD_CHUNK = CHUNK
D_IN_PROJ = 3 * D_A + D_B + CONV_DIM + SSM_HEADS
SPLITS = (D_A, 2 * D_A, 3 * D_A, 3 * D_A + D_B, 3 * D_A + D_B + CONV_DIM)
EPS = 1e-5

kernel_name = "hybrid_gmlp_ssd_streaming_step"


def rms_norm(x, w):
    xf = x.astype(jnp.float32)
    y = xf * lax.rsqrt(jnp.mean(xf * xf, axis=-1, keepdims=True) + EPS)
    return y.astype(x.dtype) * w


def layer_norm(x, w, b):
    xf = x.astype(jnp.float32)
    mu = jnp.mean(xf, axis=-1, keepdims=True)
    var = jnp.mean(jnp.square(xf - mu), axis=-1, keepdims=True)
    return ((xf - mu) * lax.rsqrt(var + EPS)).astype(x.dtype) * w + b


def gmlp_branch(u, v, g, ln_w, ln_b, ws, bs, chunk_len):
    bsz, seqlen, _ = u.shape
    n_chunks = seqlen // chunk_len
    vn = layer_norm(v, ln_w, ln_b)
    blk = jnp.arange(GMLP_CHUNK) // CHUNK
    mask = (blk[None, :] <= blk[:, None])[:chunk_len, :chunk_len]
    w_mask = jnp.where(mask[None], ws[:, :chunk_len, :chunk_len], 0.0)
    vr = vn.reshape(bsz, n_chunks, chunk_len, GMLP_HEADS, GMLP_HEAD_DIM)
    mixed = jnp.einsum('hts,bcshd->bcthd', w_mask, vr) + bs[:, :chunk_len].T[:, :, None]
    ur = u.reshape(bsz, n_chunks, chunk_len, GMLP_HEADS, GMLP_HEAD_DIM)
    out = (ur * mixed).reshape(bsz, seqlen, D_A) * jax.nn.silu(g)
    return out, vn


def causal_conv(xbc, conv_state, w, b):
    seqlen = xbc.shape[1]
    xpad = jnp.concatenate([conv_state, xbc], axis=1)
    y = xpad[:, 0:seqlen] * w[0]
    for k in range(1, CONV_W):
        y = y + xpad[:, k:k + seqlen] * w[k]
    return jax.nn.silu(y + b), xpad[:, -(CONV_W - 1):]


def ssd_scan(x, dt, a, bm, cm, h0, chunk):
    bsz, seqlen, n_heads, hd = x.shape
    nc = seqlen // chunk
    e = n_heads // SSM_GROUPS
    x = x.reshape(bsz, nc, chunk, SSM_GROUPS, e, hd)
    dt = dt.reshape(bsz, nc, chunk, SSM_GROUPS, e)
    bm = bm.reshape(bsz, nc, chunk, SSM_GROUPS, SSM_STATE)
    cm = cm.reshape(bsz, nc, chunk, SSM_GROUPS, SSM_STATE)
    a_cum = jnp.cumsum(dt.astype(jnp.float32) * a.reshape(SSM_GROUPS, e).astype(jnp.float32), axis=2)
    seg = a_cum[:, :, :, None] - a_cum[:, :, None]
    causal = jnp.tril(jnp.ones((chunk, chunk), dtype=bool))[:, :, None, None]
    l_dec = jnp.exp(jnp.where(causal, seg, -jnp.inf)).astype(x.dtype)
    xdt = x * dt[..., None]
    cb = jnp.einsum('bclgn,bcsgn->bclsg', cm, bm)
    y_diag = jnp.einsum('bclsg,bclsge,bcsgep->bclgep', cb, l_dec, xdt)
    decay_to_end = jnp.exp(a_cum[:, :, -1:] - a_cum).astype(x.dtype)
    states = jnp.einsum('bclgn,bclge,bclgep->bcgepn', bm, decay_to_end, xdt)
    chunk_decay = jnp.exp(a_cum[:, :, -1]).astype(x.dtype)

    def step(h, inp):
        s, d = inp
        return h * d[..., None, None] + s, h

    h_init = h0.reshape(bsz, SSM_GROUPS, e, hd, SSM_STATE)
    h_last, h_in = lax.scan(step, h_init, (jnp.moveaxis(states, 1, 0), jnp.moveaxis(chunk_decay, 1, 0)))
    h_in = jnp.moveaxis(h_in, 0, 1)
    y_off = jnp.einsum('bclgn,bcgepn,bclge->bclgep', cm, h_in, jnp.exp(a_cum).astype(x.dtype))
    y = (y_diag + y_off).reshape(bsz, seqlen, n_heads, hd)
    return y, h_last.reshape(bsz, n_heads, hd, SSM_STATE)


def mamba_branch(z, xbc, dt_raw, conv_state, h0, conv_w, conv_b, dt_bias, a_log, d_skip, norm_w, chunk):
    bsz, seqlen, _ = z.shape
    xbc, new_conv = causal_conv(xbc, conv_state, conv_w, conv_b)
    xs = xbc[..., :D_B].reshape(bsz, seqlen, SSM_HEADS, SSM_HEAD_DIM)
    bm = xbc[..., D_B:D_B + GN].reshape(bsz, seqlen, SSM_GROUPS, SSM_STATE)
    cm = xbc[..., D_B + GN:].reshape(bsz, seqlen, SSM_GROUPS, SSM_STATE)
    dt = jax.nn.softplus(dt_raw + dt_bias)
    a = -jnp.exp(a_log)
    y, h_last = ssd_scan(xs, dt, a, bm, cm, h0, chunk)
    y = (y + xs * d_skip[:, None]).reshape(bsz, seqlen, D_B)
    y = rms_norm(y * jax.nn.silu(z), norm_w)
    return y, h_last, new_conv


def mixer_layer(x, conv_state, h0, gmlp_chunk, ssd_chunk, norm_in_w, w_in, gmlp_ln_w, gmlp_ln_b,
                gmlp_ws, gmlp_bs, conv_w, conv_b, dt_bias, a_log, d_skip, ssm_norm_w, w_out):
    h = rms_norm(x, norm_in_w) @ w_in
    u, v, g, z, xbc, dt_raw = jnp.split(h, SPLITS, axis=-1)
    a_out, vn = gmlp_branch(u, v, g, gmlp_ln_w, gmlp_ln_b, gmlp_ws, gmlp_bs, gmlp_chunk)
    b_out, h_last, new_conv = mamba_branch(z, xbc, dt_raw, conv_state, h0, conv_w, conv_b,
                                           dt_bias, a_log, d_skip, ssm_norm_w, ssd_chunk)
    y = x + jnp.concatenate([a_out, b_out], axis=-1) @ w_out
    return y, vn, h_last, new_conv


def setup_inputs(seed: int = 0) -> dict:
    key = jax.random.key(seed)
    ks = jax.random.split(key, 20)
    nrm = jax.random.normal
    dt0 = jnp.exp(jax.random.uniform(ks[12], (DEPTH, SSM_HEADS)) * (np.log(0.1) - np.log(0.001)) + np.log(0.001))
    return {
        "x_prompt": nrm(ks[0], (BATCH, SEQ, D_MODEL), jnp.float32),
        "x_sample": nrm(ks[1], (DEC_BATCH, DEC_SEQ, D_MODEL), jnp.float32),
        "state_ssm": 0.1 * nrm(ks[2], (DEPTH, DEC_BATCH, SSM_HEADS, SSM_HEAD_DIM, SSM_STATE), jnp.float32),
        "state_conv": nrm(ks[3], (DEPTH, DEC_BATCH, CONV_W - 1, CONV_DIM), jnp.float32),
        "norm_in_w": 1.0 + 0.02 * nrm(ks[4], (DEPTH, D_MODEL), jnp.float32),
        "w_in": nrm(ks[5], (DEPTH, D_MODEL, D_IN_PROJ), jnp.float32) * D_MODEL ** -0.5,
        "gmlp_ln_w": 1.0 + 0.02 * nrm(ks[6], (DEPTH, D_A), jnp.float32),
        "gmlp_ln_b": 0.02 * nrm(ks[7], (DEPTH, D_A), jnp.float32),
        "gmlp_ws": nrm(ks[8], (DEPTH, GMLP_HEADS, GMLP_CHUNK, GMLP_CHUNK), jnp.float32) * GMLP_CHUNK ** -0.5,
        "gmlp_bs": 1.0 + 0.02 * nrm(ks[9], (DEPTH, GMLP_HEADS, GMLP_CHUNK), jnp.float32),
        "conv_w": nrm(ks[10], (DEPTH, CONV_W, CONV_DIM), jnp.float32) * CONV_W ** -0.5,
        "conv_b": 0.02 * nrm(ks[11], (DEPTH, CONV_DIM), jnp.float32),
        "dt_bias": dt0 + jnp.log(-jnp.expm1(-dt0)),
        "a_log": jnp.log(jax.random.uniform(ks[13], (DEPTH, SSM_HEADS), minval=1.0, maxval=16.0)),
        "d_skip": 1.0 + 0.02 * nrm(ks[14], (DEPTH, SSM_HEADS), jnp.float32),
        "ssm_norm_w": 1.0 + 0.02 * nrm(ks[15], (DEPTH, D_B), jnp.float32),
        "w_out": nrm(ks[16], (DEPTH, D_MIX, D_MODEL), jnp.float32) * D_MIX ** -0.5,
        "norm_f_w": 1.0 + 0.02 * nrm(ks[17], (D_MODEL,), jnp.float32),
    }


def reference(x_prompt, x_sample, state_ssm, state_conv, norm_in_w, w_in, gmlp_ln_w, gmlp_ln_b,
              gmlp_ws, gmlp_bs, conv_w, conv_b, dt_bias, a_log, d_skip, ssm_norm_w, w_out, norm_f_w):
    yp, ys = x_prompt, x_sample
    bp, dec_len = yp.shape[0], ys.shape[1]
    ssm_p, conv_p, ssm_s, conv_s, v_s = [], [], [], [], []
    for i in range(DEPTH):
        lw = (norm_in_w[i], w_in[i], gmlp_ln_w[i], gmlp_ln_b[i], gmlp_ws[i], gmlp_bs[i], conv_w[i],
              conv_b[i], dt_bias[i], a_log[i], d_skip[i], ssm_norm_w[i], w_out[i])
        conv0 = jnp.zeros((bp, CONV_W - 1, CONV_DIM), yp.dtype)
        h0 = jnp.zeros((bp, SSM_HEADS, SSM_HEAD_DIM, SSM_STATE), yp.dtype)
        yp, _, hp, cp = mixer_layer(yp, conv0, h0, GMLP_CHUNK, SSD_CHUNK, *lw)
        ys, vs, hs, cs = mixer_layer(ys, state_conv[i], state_ssm[i], dec_len, dec_len, *lw)
        ssm_p.append(hp)
        conv_p.append(cp)
        ssm_s.append(hs)
        conv_s.append(cs)
        v_s.append(vs)
    y_prompt = rms_norm(yp, norm_f_w)
    y_sample = rms_norm(ys, norm_f_w)
    return (y_prompt, y_sample, jnp.stack(ssm_p), jnp.stack(conv_p), jnp.stack(ssm_s), jnp.stack(conv_s), jnp.stack(v_s))
```

```python
import contextlib
import numpy as np
import concourse.bass as bass
import concourse.mybir as mybir
from concourse.bass_utils import run_bass_kernel_spmd

F32 = mybir.dt.float32
BF16 = mybir.dt.bfloat16
AF = mybir.ActivationFunctionType
ALU = mybir.AluOpType

D = 1024
SEQ = 4096
NS = 4
TS = 64
DIN = 5648
CONV = 1536
EPS = 1e-5
OU, OV, OG, OZ, OX, ODT = 0, 1024, 2048, 3072, 4096, 5632


class Buf:
    def __init__(self, name, excl=False):
        self.name = name
        self.w = set()
        self.r = set()
        self.xr = set()
        self.excl = excl


class Unit:
    __slots__ = ("eng", "fns", "reads", "writes", "cost", "deps", "key", "idx", "lat", "tab", "line")

    def __init__(self, eng, key=None):
        self.eng = eng
        self.fns = []
        self.reads = []
        self.writes = []
        self.cost = 0.0
        self.deps = set()
        self.key = key
        self.lat = 0.0
        self.tab = None
        self.line = 0


class Sched:
    XLAT = 0.8
    COST = {"pe": (0.015, 1.0 / 2300), "act": (0.2, 1.0 / 1200), "dve": (0.1, 1.0 / 920), "pool": (0.1, 1.0 / 480)}

    def __init__(self, nc, es):
        self.nc = nc
        self.es = es
        self.engs = {"pe": nc.tensor, "act": nc.scalar, "dve": nc.vector, "pool": nc.gpsimd, "sp": nc.sync}
        self.sem = {}
        self.cnt = {}
        self.waited = {e: {} for e in self.engs}
        for e in ("pe", "act", "dve", "pool"):
            self.newsem(e)
        self.units = []
        self.pending = {}
        self.last_dma = {}
        self.sigval = {}

    def newsem(self, key):
        self.sem[key] = self.es.enter_context(self.nc.semaphore("s_" + key))
        self.cnt[key] = 0

    def _close(self, u):
        deps = set()
        for b in u.reads:
            deps |= b.w
            if b.excl:
                deps |= {x for x in b.xr if self.units[x].eng != u.eng}
        for b in u.writes:
            deps |= b.w
            deps |= b.r
            deps |= b.xr
        u.idx = len(self.units)
        u.deps = deps
        for b in u.reads:
            if b.excl:
                b.xr.add(u.idx)
            else:
                b.r.add(u.idx)
        for b in u.writes:
            b.w = {u.idx}
            b.r = set()
            b.xr = set()
        self.units.append(u)

    def op(self, e, fn, reads=(), writes=(), signal=True, n=None, tab=None):
        u = self.pending.get(e)
        if u is None:
            u = Unit(e)
            self.pending[e] = u
        u.fns.append(fn)
        if not u.line:
            import sys as _sys
            u.line = _sys._getframe(2).f_lineno
        if tab is not None:
            u.tab = tab
        u.reads += list(reads)
        u.writes += list(writes)
        c0, c1 = self.COST[e]
        u.cost += c0 + c1 * (n if n is not None else (512 if e == "pe" else 1024))
        if signal:
            del self.pending[e]
            self._close(u)

    def dma(self, key, out, in_, reads=(), writes=(), **kw):
        if key not in self.sem:
            self.newsem(key)
        u = Unit("sp", key)
        import sys as _sys
        u.line = _sys._getframe(1).f_lineno
        u.fns.append(lambda: self.nc.sync.dma_start(out=out, in_=in_, **kw))
        u.reads = list(reads)
        u.writes = list(writes)
        u.cost = 0.15
        u.lat = 4.0
        self._close(u)
        if key in self.last_dma:
            u.deps.add(self.last_dma[key])
        self.last_dma[key] = u.idx

    def _wait(self, e, key, val):
        if val <= 0 or self.waited[e].get(key, 0) >= val:
            return
        self.engs[e].wait_ge(self.sem[key], val)
        self.waited[e][key] = val

    def flush(self):
        assert not self.pending, list(self.pending)
        units = self.units
        n = len(units)
        XL, SL, TSW, EPS_T = self.XLAT, 0.2, 0.0, 1.4
        users = [[] for _ in range(n)]
        for u in units:
            for d in u.deps:
                users[d].append(u.idx)
        bl = [0.0] * n
        for i in range(n - 1, -1, -1):
            u = units[i]
            m = 0.0
            for v in users[i]:
                c = bl[v] + (SL if units[v].eng == u.eng else XL)
                if c > m:
                    m = c
            bl[i] = u.cost + u.lat + m
        ndep = [len(u.deps) for u in units]
        rdy = [0.0] * n
        fin = [0.0] * n
        free = {e: 0.0 for e in self.engs}
        rel = {e: set() for e in self.engs}
        for i in range(n):
            if ndep[i] == 0:
                rel[units[i].eng].add(i)
        cur_tab = None
        order = []
        nleft = n
        while nleft:
            best_e, best_t = None, None
            for e, ss in rel.items():
                if not ss:
                    continue
                te = max(free[e], min(rdy[i] for i in ss))
                if best_t is None or te < best_t:
                    best_e, best_t = e, te
            e = best_e
            cand = [i for i in rel[e] if rdy[i] <= best_t + EPS_T]
            if e == "act" and cur_tab is not None:
                same = [i for i in cand if units[i].tab in (None, cur_tab)]
                if same:
                    cand = same
            i = max(cand, key=lambda j: (bl[j], -j))
            u = units[i]
            rel[e].remove(i)
            st = max(free[e], rdy[i])
            c = u.cost
            if u.tab is not None:
                if cur_tab is not None and u.tab != cur_tab:
                    c += TSW
                cur_tab = u.tab
            free[e] = st + c
            fin[i] = st + c + u.lat
            order.append(i)
            nleft -= 1
            for v in users[i]:
                t = fin[i] + (SL if units[v].eng == e else XL)
                if t > rdy[v]:
                    rdy[v] = t
                ndep[v] -= 1
                if ndep[v] == 0:
                    rel[units[v].eng].add(v)
        assert len(order) == n
        self.est_total = max(fin) if fin else 0.0
        for i in order:
            u = units[i]
            e = u.eng
            for d in sorted(u.deps):
                if units[d].eng == e and e == "pe":
                    continue
                key, val = self.sigval[d]
                self._wait(e, key, val)
            if e == "sp":
                ins = u.fns[0]()
                self.cnt[u.key] += 16
                ins.then_inc(self.sem[u.key], 16)
                self.sigval[i] = (u.key, self.cnt[u.key])
            else:
                ins = None
                for f in u.fns:
                    ins = f()
                self.cnt[e] += 1
                ins.then_inc(self.sem[e], 1)
                self.sigval[i] = (e, self.cnt[e])
        self.units = []


def build_nc(SEQ=SEQ, NS=NS):
    nc = bass.Bass("TRN2", target_bir_lowering=False)

    def din(name, shape):
        return nc.dram_tensor(name, list(shape), F32, kind="ExternalInput").ap()

    def dout(name, shape):
        return nc.dram_tensor(name, list(shape), F32, kind="ExternalOutput").ap()

    xp = din("xp", (SEQ, D))
    xs = din("xs", (NS, TS, D))
    sssm = din("sssm", (NS, 1024, 128))
    sconv = din("sconv", (NS, 3, CONV))
    nin = din("norm_in_w", (D,))
    w_in = din("w_in", (D, DIN))
    lnw = din("gmlp_ln_w", (D,))
    lnb = din("gmlp_ln_b", (D,))
    gws = din("gmlp_ws", (8, 128, 128))
    gbs = din("gmlp_bs", (8, 128))
    cvw = din("conv_w", (4, CONV))
    cvb = din("conv_b", (CONV,))
    dtb = din("dt_bias", (16,))
    alog = din("a_log", (16,))
    dsk = din("d_skip", (16,))
    snw = din("ssm_norm_w", (D,))
    w_out = din("w_out", (2048, D))
    nfw = din("norm_f_w", (D,))

    yp = dout("yp", (SEQ, D))
    ys = dout("ys", (NS, TS, D))
    ossm_p = dout("ossm_p", (1024, 128))
    oconv_p = dout("oconv_p", (3, CONV))
    ossm_s = dout("ossm_s", (NS, 1024, 128))
    oconv_s = dout("oconv_s", (NS, 3, CONV))
    ov_s = dout("ov_s", (NS, TS, D))

    es = contextlib.ExitStack()
    with es:
        S = Sched(nc, es)

        def sb(name, shape, dt):
            return es.enter_context(nc.sbuf_tensor(name, list(shape), dt))

        def ps(name, shape, dt):
            return es.enter_context(nc.psum_tensor(name, list(shape), dt))

        Win = sb("Win", (128, 8, DIN), BF16)
        Wout = sb("Wout", (128, 16, D), BF16)
        identF = sb("identF", (128, 128), F32)
        identB = sb("identB", (128, 128), BF16)
        triF = sb("triF", (128, 128), F32)
        onesF = sb("onesF", (128, 128), F32)
        NEG4 = sb("NEG4", (128, 512), BF16)
        WmT = sb("WmT", (128, 8, 128), BF16)
        lnw_bc = sb("lnw_bc", (128, D), F32)
        lnb_bc = sb("lnb_bc", (128, D), F32)
        nf_bc = sb("nf_bc", (128, D), F32)
        diagD = sb("diagD", (128, 8, 128), BF16)
        cst = sb("cst", (128, 256), F32)
        cwb = cst[:, 64:124].rearrange("p (b k) -> p b k", k=5)
        RmB = sb("RmB", (64, 16), BF16)
        lhsT_d = sb("lhsT_d", (128, 128), BF16)
        XF = sb("XF", (128, D), F32)
        Ft = sb("Ft", (128, D), F32)
        hT = sb("hT", (128, D), F32)
        H1x = sb("H1x", (128, D), BF16)
        xT = sb("xT", (128, D), BF16)
        F1b = [sb("F1b%d" % i, (128, D), BF16) for i in range(2)]
        F3b = [sb("F3b%d" % i, (128, D), BF16) for i in range(2)]
        H2 = [sb("H2_%d" % i, (128, D), BF16) for i in range(2)]
        xbc_bf = sb("xbc_bf", (128, CONV), BF16)
        H1 = sb("H1", (128, D), BF16)
        H3a = sb("H3a", (128, D), BF16)
        H3b = sb("H3b", (128, D), BF16)
        H4 = sb("H4", (128, D), BF16)
        aT = sb("aT", (128, D), BF16)
        bT = sb("bT", (128, D), BF16)
        hTb = sb("hTb", (128, D), BF16)
        raw = sb("raw", (128, 12 * 131), BF16)
        cvA = sb("cvA", (128, 512), F32)
        cvB = sb("cvB", (128, 512), F32)
        cvD = sb("cvD", (128, 512), F32)
        actT2 = [sb("actT%d" % i, (128, 12 * 128), BF16) for i in range(2)]
        Rfl = sb("Rfl", (128, 2048), BF16)
        CBm = sb("CBm", (128, 256), BF16)
        Btm = sb("Btm", (128, 256), BF16)
        sm = sb("sm", (128, 384), F32)
        smb = sb("smb", (64, 384), BF16)

        P_IP = ps("P_IP", (128, 1024), F32)
        P_X = ps("P_X", (128, 1024), F32)
        P_Y = ps("P_Y", (128, 1024), F32)
        P_T = ps("P_T", (128, 1024), BF16)
        P_S = ps("P_S", (128, 512), F32)

        B = {n: Buf(n) for n in (
            "WinA", "WinD", "const", "XF", "Ft", "hT", "H1x", "xT", "F1b0", "F1b1", "F3b0", "F3b1", "H2_0", "H2_1",
            "xbc", "actT0", "actT1", "H1", "H3a", "H3b", "H4", "aT", "bT", "hTb", "raw", "cvA", "cvB", "cvD0", "cvD1", "cvD2", "cvD3", "Rfl", "CBm", "Btm",
            "ip0", "ip1", "PX", "PY", "PT", "PS",
            "ss1", "rs1", "ss2", "rs2", "vst", "dt0", "dt1", "A4", "Ee", "dte", "cd", "V", "lhsT_d")}
        for n in ("ip0", "ip1", "PX", "PY", "PT", "PS"):
            B[n].excl = True

        S.newsem("const")
        nconst = [0]

        def cdma(out, in_, **kw):
            nc.sync.dma_start(out=out, in_=in_, **kw).then_inc(S.sem["const"], 16)
            nconst[0] += 16

        cdma(Ft[0:8, 768:896], gbs[:, :])
        cdma(Ft[8:16, 768:896], nin.rearrange("(k p) -> k p", p=128))
        cdma(Ft[16:24, 768:896], snw.rearrange("(k p) -> k p", p=128))
        cdma(Ft[0:4, 0:768], cvw[:, 0:768])
        cdma(hT[0:4, 0:768], cvw[:, 768:CONV])
        cdma(Ft[4:5, 0:768], cvb[0:768].rearrange("(o n) -> o n", o=1))
        cdma(hT[4:5, 0:768], cvb[768:CONV].rearrange("(o n) -> o n", o=1))
        cdma(lnw_bc[:], lnw.partition_broadcast(128))
        cdma(lnb_bc[:], lnb.partition_broadcast(128))
        cdma(nf_bc[:], nfw.partition_broadcast(128))
        cdma(cst[:, 32:48], alog.partition_broadcast(128))
        cdma(cst[:, 48:64], dtb.partition_broadcast(128))
        dsk2 = dsk.rearrange("(j t) -> t j", t=2)
        cdma(cst[0:64, 24:32], dsk2[0].partition_broadcast(64), allow_slow_non_contiguous=True)
        cdma(cst[64:128, 24:32], dsk2[1].partition_broadcast(64), allow_slow_non_contiguous=True)
        for h in range(8):
            cdma(XF[:, h * 128:(h + 1) * 128], gws[h])
        for e in ("pe", "act", "dve", "pool"):
            S.engs[e].wait_ge(S.sem["const"], nconst[0])
            S.waited[e]["const"] = nconst[0]
        S.waited["sp"]["const"] = 0
        S.cnt["const"] = nconst[0]

        cB = B["const"]
        def pool(fn, reads=(), writes=(), n=None):
            return S.op("pool", fn, reads, writes, True, n)

        def dve(fn, reads=(), writes=(), n=None):
            return S.op("dve", fn, reads, writes, True, n)

        def act(fn, reads=(), writes=(), n=None, tab=None):
            return S.op("act", fn, reads, writes, True, n, tab)

        def pe(fn, reads=(), writes=(), signal=True, n=None):
            return S.op("pe", fn, reads, writes, signal, n)

        def asel(t, pattern, cmp, base, cm):
            pool(lambda: nc.gpsimd.affine_select(out=t, in_=t, pattern=pattern, compare_op=cmp, fill=0.0,
                                                 base=base, channel_multiplier=cm), [cB], [cB])

        for t in (identF, triF, onesF):
            pool(lambda t=t: nc.gpsimd.memset(t[:], 1.0), [], [cB])
        asel(identF[:], [[-1, 128]], ALU.is_equal, 0, 1)
        B["cNin"] = Buf("cNin")
        pe(lambda: nc.tensor.matmul(P_S[:, 0:24], lhsT=Ft[0:24, 768:896], rhs=identF[0:24, 0:24], start=True, stop=True),
           [cB], [B["PS"]])
        dve(lambda: nc.vector.tensor_copy(out=cst[:, 0:24], in_=P_S[:, 0:24]), [B["PS"]], [cB, B["cNin"]])
        bsT = cst[:, 0:8]
        nin_pp = cst[:, 8:16]
        snw_pp = cst[:, 16:24]
        asel(triF[:], [[1, 128]], ALU.is_ge, 0, -1)
        RmF = cst[0:64, 128:144]
        tmpm = cst[0:64, 144:160]
        pool(lambda: nc.gpsimd.memset(RmF, 0.0), [], [cB])
        for q in range(4):
            pool(lambda: nc.gpsimd.memset(tmpm, 1.0), [], [cB])
            asel(tmpm, [[-1, 16]], ALU.is_equal, -16 * q, 1)
            pool(lambda: nc.gpsimd.tensor_tensor(out=RmF, in0=RmF, in1=tmpm, op=ALU.add), [cB], [cB])
        mh = cst[0:64, 125:126]
        ml = cst[0:64, 126:127]
        t1 = cst[0:64, 160:161]
        pool(lambda: nc.gpsimd.memset(mh, 1.0), [], [cB])
        asel(mh, [[0, 1]], ALU.is_ge, 15, -1)
        pool(lambda: nc.gpsimd.memset(t1, 1.0), [], [cB])
        asel(t1, [[0, 1]], ALU.is_ge, -32, 1)
        asel(t1, [[0, 1]], ALU.is_ge, 47, -1)
        pool(lambda: nc.gpsimd.tensor_tensor(out=mh, in0=mh, in1=t1, op=ALU.add), [cB], [cB])
        pool(lambda: nc.gpsimd.memset(ml, 1.0), [], [cB])
        pool(lambda: nc.gpsimd.tensor_tensor(out=ml, in0=ml, in1=mh, op=ALU.subtract), [cB], [cB])
        pool(lambda: nc.gpsimd.memset(cst[:, 124:125], -0.5), [], [cB])
        pool(lambda: nc.gpsimd.memset(lhsT_d[0:64, :], 1.0), [], [cB])
        pool(lambda: nc.gpsimd.memset(lhsT_d[64:128, :], 0.0), [], [cB])
        pool(lambda: nc.gpsimd.memset(Rfl[64:128, :], 0.0), [], [B["Rfl"]])
        neghalf = cst[:, 124:125]
        epsc = cst[:, 127:128]
        pool(lambda: nc.gpsimd.memset(epsc, EPS), [], [cB])

        dve(lambda: nc.vector.tensor_copy(out=identB[:], in_=identF[:]), [cB], [cB])
        dve(lambda: nc.vector.tensor_scalar(out=hT[:, 768:896], in0=triF[:], scalar1=-1.0, scalar2=30000.0,
                                            op0=ALU.add, op1=ALU.mult), [cB], [cB])
        dve(lambda: nc.vector.tensor_copy(out=NEG4[:, :].rearrange("p (h l) -> p h l", l=128),
                                          in_=hT[:, 768:896].unsqueeze(1).to_broadcast([128, 4, 128])), [cB], [cB])
        dve(lambda: nc.vector.tensor_copy(out=RmB[:], in_=RmF), [cB], [cB])
        act(lambda: nc.scalar.activation(out=cst[:, 32:48], in_=cst[:, 32:48], func=AF.Exp), [cB], [cB], tab="exp")
        dve(lambda: nc.vector.tensor_scalar(out=cst[:, 32:48], in0=cst[:, 32:48], scalar1=-1.0, scalar2=None,
                                            op0=ALU.mult), [cB], [cB])
        a_bc = cst[:, 32:48]
        dtb_bc = cst[:, 48:64]
        for blk in range(12):
            src = Ft if blk < 6 else hT
            cc = (blk % 6) * 128
            pe(lambda src=src, cc=cc, blk=blk: nc.tensor.matmul(P_S[:, 64 + blk * 5:64 + (blk + 1) * 5], lhsT=src[0:5, cc:cc + 128],
                                                                 rhs=identF[0:5, 0:5], start=True, stop=True),
               [cB], [B["PS"]])
        dve(lambda: nc.vector.tensor_copy(out=cst[:, 64:124], in_=P_S[:, 64:124]), [B["PS"]], [cB])
        for j in range(8):
            dve(lambda j=j: nc.vector.tensor_scalar(out=diagD[:, j, :], in0=identF[:], scalar1=cst[:, 24 + j:25 + j],
                                                    scalar2=None, op0=ALU.mult), [cB], [cB])
        for h in range(8):
            pe(lambda h=h: nc.tensor.matmul(P_X[:, (h % 4) * 128:(h % 4 + 1) * 128] if h < 4 else
                                                       P_Y[:, (h % 4) * 128:(h % 4 + 1) * 128],
                                                       lhsT=XF[:, h * 128:(h + 1) * 128], rhs=identF[:],
                                                       start=True, stop=True),
               [cB], [B["PX"] if h < 4 else B["PY"]])
        dve(lambda: nc.vector.tensor_copy(out=WmT[:, 0:4, :], in_=P_X[:, 0:512].rearrange("p (h t) -> p h t", t=128)),
            [B["PX"]], [cB])
        dve(lambda: nc.vector.tensor_copy(out=WmT[:, 4:8, :], in_=P_Y[:, 0:512].rearrange("p (h t) -> p h t", t=128)),
            [B["PY"]], [cB])
        dve(lambda: nc.vector.memset(WmT[64:128, :, 0:64], 0.0), [cB], [cB])
        for b in [B["XF"], B["Ft"], B["hT"]]:
            b.r = set(cB.r)
            b.w = set(cB.w)

        for nm in ["Wi%d%s" % (pc, e) for pc in range(12) for e in "AD"] + ["WoA", "WoD", "FtL", "FtH", "hTL", "hTH"]:
            B[nm] = Buf(nm)
        f32v = lambda t: t[:, :].bitcast(F32)[:, 0:512]
        stgs = [(cvA[:, 0:512], [B["cvA"]]), (cvB[:, 0:512], [B["cvB"]]), (Ft[:, 0:512], [B["Ft"]]), (hT[:, 0:512], [B["hT"]]),
                (cvD[:, 0:512], [B["cvD0"], B["cvD1"], B["cvD2"], B["cvD3"]]),
                (f32v(H3a), [B["H3a"]]), (f32v(H3b), [B["H3b"]]), (f32v(aT), [B["aT"]]), (f32v(bT), [B["bT"]]),
                (f32v(H4), [B["H4"]]), (f32v(H1), [B["H1"]])]
        wi = [0]

        def wload(dst, src_ap, ncols, scale_ap, wname):
            st, stb = stgs[wi[0] % len(stgs)]
            eng = ("act", "dve")[wi[0] % 2]
            S.dma("stg%d" % (wi[0] % len(stgs)), st[:, 0:ncols], src_ap, [], stb)
            WB = B[wname + ("A" if eng == "act" else "D")]
            if eng == "act":
                if scale_ap is None:
                    act(lambda: nc.scalar.copy(out=dst, in_=st[:, 0:ncols]), stb + [B["cNin"]], [WB], n=ncols)
                else:
                    act(lambda: nc.scalar.activation(out=dst, in_=st[:, 0:ncols], func=AF.Identity, scale=scale_ap),
                        stb + [B["cNin"]], [WB], n=ncols)
            else:
                if scale_ap is None:
                    dve(lambda: nc.vector.tensor_copy(out=dst, in_=st[:, 0:ncols]), stb + [B["cNin"]], [WB], n=ncols // 2)
                else:
                    dve(lambda: nc.vector.tensor_scalar(out=dst, in0=st[:, 0:ncols], scalar1=scale_ap, scalar2=None,
                                                        op0=ALU.mult), stb + [B["cNin"]], [WB], n=ncols // 2)
            wi[0] += 1

        for pc in (8, 9, 10, 11, 6, 7, 4, 5, 0, 1, 2, 3):
            c0 = pc * 512
            w = min(512, DIN - c0)
            for k in range(8):
                wload(Win[:, k, c0:c0 + w], w_in[k * 128:(k + 1) * 128, c0:c0 + w], w, nin_pp[:, k:k + 1], "Wi%d" % pc)
        for k in range(16):
            for cg in range(2):
                wload(Wout[:, k, cg * 512:(cg + 1) * 512], w_out[k * 128:(k + 1) * 128, cg * 512:(cg + 1) * 512], 512,
                      None if k < 8 else snw_pp[:, k - 8:k - 7], "Wo")

        def rstd_from_ss(T, col, sB, rB):
            act(lambda: nc.scalar.activation(out=sm[:T, col + 1:col + 2], in_=sm[:T, col:col + 1], func=AF.Ln,
                                             scale=1.0 / 1024, bias=epsc[:T]), [sB, cB], [rB], tab="exp")
            act(lambda: nc.scalar.activation(out=sm[:T, col + 1:col + 2], in_=sm[:T, col + 1:col + 2], func=AF.Exp,
                                             scale=-0.5), [rB], [rB], tab="exp")

        v3 = lambda ap, inner: ap.rearrange("p (a b) -> p a b", b=inner)
        GROUPS = [("x", OX, 512), ("x", OX + 512, 512), ("x", OX + 1024, 512), ("d", ODT, 16),
                  ("z", OZ, 512), ("z", OZ + 512, 512), ("g", OG, 512), ("g", OG + 512, 512),
                  ("u", OU, 512), ("u", OU + 512, 512), ("v", OV, 512), ("v", OV + 512, 512)]
        KOFF = {"g": OG, "u": OU, "v": OV, "z": OZ, "x": OX, "d": ODT}
        W = [B["WinA"], B["WinD"]]
        rstate = {}

        def phase1(i, tt, next_load):
            T, last, conv_dst, v_dst = tt["T"], tt["last"], tt["conv_dst"], tt["v_dst"]
            first, init_conv = tt["first"], tt["init_conv"]
            p = i % 2
            F1t, F1B = F1b[p], B["F1b%d" % p]
            F3t, F3B = F3b[p], B["F3b%d" % p]
            H2t, H2B = H2[p], B["H2_%d" % p]
            xbt, xbB = xbc_bf, B["xbc"]
            actT, actB = actT2[p], B["actT%d" % p]
            rawv = v3(raw[:, 0:12 * (3 + T)], 3 + T)
            actv = v3(actT[:, 0:12 * T], T)
            dtc = 16 if p == 0 else 176
            dtB = B["dt%d" % p]
            chunks = []

            def c_init():
                if init_conv is None:
                    pool(lambda: nc.gpsimd.memset(raw[:], 0.0), [], [B["raw"]])
                    return
                S.dma("ldcs", cvA[0:3, 0:512], init_conv[:, 0:512], [], [B["cvA"]])
                S.dma("ldcs2", cvB[0:3, 0:512], init_conv[:, 512:1024], [], [B["cvB"]])
                act(lambda: nc.scalar.copy(out=xbt[0:3, 0:512], in_=cvA[0:3, 0:512]), [B["cvA"]], [xbB])
                act(lambda: nc.scalar.copy(out=xbt[0:3, 512:1024], in_=cvB[0:3, 0:512]), [B["cvB"]], [xbB])
                S.dma("ldcs3", cvD[0:3, 0:512], init_conv[:, 1024:1536], [], [B["cvD0"], B["cvD1"], B["cvD2"], B["cvD3"]])
                act(lambda: nc.scalar.copy(out=xbt[0:3, 1024:1536], in_=cvD[0:3, 0:512]), [B["cvD0"], B["cvD1"], B["cvD2"], B["cvD3"]], [xbB])
                for blk in range(12):
                    pe(lambda blk=blk: nc.tensor.transpose(out=P_T[:, blk * 4:blk * 4 + 3], in_=xbt[0:3, blk * 128:(blk + 1) * 128],
                                                           identity=identB[0:3, 0:3]), [xbB, cB], [B["PT"]], signal=(blk == 11), n=8)
                act(lambda: nc.scalar.copy(out=rawv[:, :, 0:3], in_=v3(P_T[:, 0:48], 4)[:, :, 0:3]), [B["PT"]], [B["raw"]])
            if first:
                chunks.append(c_init)

            def cA():
                act(lambda: nc.scalar.activation(out=H1x[:T, :], in_=XF[:T, :], func=AF.Square, accum_out=sm[:T, 0:1]),
                    [B["XF"]], [B["H1x"], B["ss1"]])
                rstd_from_ss(T, 0, B["ss1"], B["rs1"])
                dve(lambda: nc.vector.tensor_scalar(out=H1x[:T, :], in0=XF[:T, :], scalar1=sm[:T, 1:2], scalar2=None,
                                                    op0=ALU.mult), [B["XF"], B["rs1"]], [B["H1x"]], n=600)
                for k in range(8):
                    pe(lambda k=k: nc.tensor.transpose(out=P_T[:, k * T:(k + 1) * T], in_=H1x[:T, k * 128:(k + 1) * 128],
                                                       identity=identB[:T, :T]), [B["H1x"], cB], [B["PT"]], signal=(k == 7), n=T)
                act(lambda: nc.scalar.copy(out=xT[:, 0:8 * T], in_=P_T[:, 0:8 * T]), [B["PT"]], [B["xT"]])
                if next_load is not None:
                    next_load()
            chunks.append(cA)

            def mk_group(gi, kind, c0, w):
                def cG():
                    half = gi % 2
                    ipB = B["ip%d" % half]
                    pst = P_IP[:T, half * 512:half * 512 + w]
                    for k in range(8):
                        pe(lambda k=k: nc.tensor.matmul(pst, lhsT=xT[:, k * T:(k + 1) * T], rhs=Win[:, k, c0:c0 + w],
                                                        start=(k == 0), stop=(k == 7)),
                           [B["xT"], B["Wi%dA" % (c0 // 512)], B["Wi%dD" % (c0 // 512)]], [ipB], signal=(k == 7))
                    lc = c0 - KOFF[kind]
                    if kind == "g":
                        act(lambda: nc.scalar.activation(out=F1t[:T, lc:lc + 512], in_=pst, func=AF.Silu), [ipB], [F1B], n=512, tab="silu")
                    elif kind == "u":
                        dve(lambda: nc.vector.tensor_tensor(out=F1t[:T, lc:lc + 512], in0=pst, in1=F1t[:T, lc:lc + 512],
                                                            op=ALU.mult), [ipB, F1B], [F1B], n=512)
                    elif kind == "z":
                        act(lambda: nc.scalar.activation(out=F3t[:T, lc:lc + 512], in_=pst, func=AF.Silu), [ipB], [F3B], n=512, tab="silu")
                    elif kind == "x":
                        act(lambda: nc.scalar.copy(out=xbt[:T, lc:lc + 512], in_=pst), [ipB], [xbB], n=512)
                        if last:
                            h0 = T // 2
                            pc = lc // 512
                            stg, stB = ((H1x, B["H1x"]), (F1t, F1B), (H2t, H2B))[pc]
                            s32 = stg[:, :].bitcast(F32)
                            act(lambda: nc.scalar.copy(out=s32[h0:T, 0:512], in_=P_IP[h0:T, half * 512:half * 512 + 512]),
                                [ipB], [stB], n=512)
                            S.dma("st_cv%d" % pc, conv_dst[:, lc:lc + 512], s32[T - 3:T, 0:512], [stB], [])
                    elif kind == "d":
                        dve(lambda: nc.vector.tensor_tensor(out=sm[:T, dtc:dtc + 16], in0=pst, in1=dtb_bc[:T, :], op=ALU.add),
                            [ipB, cB], [dtB], n=16)
                        act(lambda: nc.scalar.activation(out=sm[:T, dtc:dtc + 16], in_=sm[:T, dtc:dtc + 16], func=AF.Exp),
                            [dtB], [dtB], n=16, tab="exp")
                        act(lambda: nc.scalar.activation(out=sm[:T, dtc:dtc + 16], in_=sm[:T, dtc:dtc + 16], func=AF.Ln,
                                                         bias=1.0), [dtB], [dtB], n=16, tab="exp")
                    elif kind == "v" and lc == 512:
                        pall = P_IP[:T, :]
                        both = [B["ip0"], B["ip1"]]
                        vB = B["vst"]
                        act(lambda: nc.scalar.activation(out=H1x[:T, :], in_=pall, func=AF.Identity, accum_out=sm[:T, 4:5]),
                            both, [B["H1x"], vB])
                        act(lambda: nc.scalar.activation(out=H1x[:T, :], in_=pall, func=AF.Square, accum_out=sm[:T, 5:6]),
                            both, [B["H1x"], vB])
                        dve(lambda: nc.vector.tensor_scalar(out=sm[:T, 4:6], in0=sm[:T, 4:6], scalar1=1.0 / 1024,
                                                            scalar2=None, op0=ALU.mult), [vB], [vB], n=8)
                        dve(lambda: nc.vector.tensor_tensor(out=sm[:T, 6:7], in0=sm[:T, 4:5], in1=sm[:T, 4:5], op=ALU.mult),
                            [vB], [vB], n=8)
                        dve(lambda: nc.vector.tensor_tensor(out=sm[:T, 6:7], in0=sm[:T, 5:6], in1=sm[:T, 6:7],
                                                            op=ALU.subtract), [vB], [vB], n=8)
                        act(lambda: nc.scalar.activation(out=sm[:T, 7:8], in_=sm[:T, 6:7], func=AF.Ln, bias=epsc[:T]),
                            [vB, cB], [vB], n=8, tab="exp")
                        act(lambda: nc.scalar.activation(out=sm[:T, 7:8], in_=sm[:T, 7:8], func=AF.Exp, scale=-0.5), [vB], [vB], n=8, tab="exp")
                        dve(lambda: nc.vector.scalar_tensor_tensor(out=sm[:T, 8:9], in0=sm[:T, 4:5], scalar=-1.0,
                                                                   in1=sm[:T, 7:8], op0=ALU.mult, op1=ALU.mult), [vB], [vB], n=8)
                        for hf, (cv, cvBf) in enumerate(((cvA, B["cvA"]), (cvB, B["cvB"]))):
                            cs = slice(hf * 512, hf * 512 + 512)
                            act(lambda cv=cv, cs=cs: nc.scalar.activation(out=cv[:T, 0:512], in_=P_IP[:T, cs], func=AF.Identity,
                                                                          scale=sm[:T, 7:8], bias=sm[:T, 8:9]),
                                [B["ip%d" % hf], vB], [cvBf], n=512)
                            pool(lambda cv=cv, cs=cs: nc.gpsimd.tensor_tensor(out=cv[:T, 0:512], in0=cv[:T, 0:512],
                                                                              in1=lnw_bc[:T, cs], op=ALU.mult),
                                 [cvBf, cB], [cvBf], n=512)
                            if v_dst is None:
                                pool(lambda cv=cv, cs=cs: nc.gpsimd.tensor_tensor(out=H2t[:T, cs], in0=cv[:T, 0:512],
                                                                                  in1=lnb_bc[:T, cs], op=ALU.add),
                                     [cvBf, cB], [H2B], n=512)
                            else:
                                pool(lambda cv=cv, cs=cs: nc.gpsimd.tensor_tensor(out=cv[:T, 0:512], in0=cv[:T, 0:512],
                                                                                  in1=lnb_bc[:T, cs], op=ALU.add),
                                     [cvBf, cB], [cvBf], n=512)
                                act(lambda cv=cv, cs=cs: nc.scalar.copy(out=H2t[:T, cs], in_=cv[:T, 0:512]), [cvBf], [H2B], n=512)
                                S.dma("st_v%d" % hf, v_dst[:, cs], cv[:T, 0:512], [cvBf], [])
                return cG
            def mk_convT(b0, b1):
                def f():
                    nb = b1 - b0
                    for blk in range(b0, b1):
                        pe(lambda blk=blk: nc.tensor.transpose(out=P_T[:, (blk - b0) * T:(blk - b0 + 1) * T],
                                                               in_=xbt[:T, blk * 128:(blk + 1) * 128], identity=identB[:T, :T]),
                           [xbB, cB], [B["PT"]], signal=(blk == b1 - 1), n=T)
                    act(lambda: nc.scalar.copy(out=rawv[:, b0:b1, 3:3 + T], in_=v3(P_T[:, 0:nb * T], T)), [B["PT"]], [B["raw"]])
                return f

            def mk_conv(c):
                def f():
                    if c == 0:
                        bs_ = slice(0, 4)
                        accv = v3(cvA[:, 0:4 * T], T)
                        tmpv = v3(cvB[:, 0:4 * T], T)
                        wk = lambda k: cwb[:, bs_, k:k + 1].to_broadcast([128, 4, T])
                        pool(lambda: nc.gpsimd.tensor_tensor(out=accv, in0=rawv[:, bs_, 0:T], in1=wk(0), op=ALU.mult),
                             [B["raw"], cB], [B["cvA"]], n=4 * T)
                        for k in (1, 2, 3):
                            pool(lambda k=k: nc.gpsimd.tensor_tensor(out=tmpv, in0=rawv[:, bs_, k:k + T], in1=wk(k), op=ALU.mult),
                                 [B["raw"], cB], [B["cvB"]], n=4 * T)
                            pool(lambda: nc.gpsimd.tensor_tensor(out=accv, in0=accv, in1=tmpv, op=ALU.add),
                                 [B["cvA"], B["cvB"]], [B["cvA"]], n=4 * T)
                        pool(lambda: nc.gpsimd.tensor_tensor(out=accv, in0=accv, in1=wk(4), op=ALU.add), [B["cvA"], cB], [B["cvA"]],
                             n=4 * T)
                        act(lambda: nc.scalar.activation(out=actv[:, bs_, :], in_=accv, func=AF.Silu), [B["cvA"]], [actB],
                            n=4 * T, tab="silu")
                    else:
                        for j in range(4):
                            blk = 4 * c + j
                            aj = cvD[:, j * 128:j * 128 + T]
                            aB = B["cvD%d" % j]
                            dve(lambda blk=blk, aj=aj: nc.vector.tensor_scalar(out=aj, in0=rawv[:, blk, 0:T],
                                                                               scalar1=cwb[:, blk, 0:1], scalar2=None,
                                                                               op0=ALU.mult), [B["raw"], cB], [aB], n=T // 2)
                            for k in (1, 2, 3):
                                dve(lambda blk=blk, aj=aj, k=k: nc.vector.scalar_tensor_tensor(
                                    out=aj, in0=rawv[:, blk, k:k + T], scalar=cwb[:, blk, k:k + 1], in1=aj,
                                    op0=ALU.mult, op1=ALU.add), [B["raw"], aB, cB], [aB], n=T)
                            act(lambda blk=blk, aj=aj: nc.scalar.activation(out=actv[:, blk, :], in_=aj, func=AF.Silu,
                                                                            bias=cwb[:, blk, 4:5]), [aB, cB], [actB],
                                n=T, tab="silu")
                    if c == 2:
                        pool(lambda: nc.gpsimd.tensor_copy(out=rawv[:, :, 0:3], in_=rawv[:, :, T:T + 3]), [B["raw"]], [B["raw"]],
                             n=36)
                return f

            for gi, (kind, c0, w) in enumerate(GROUPS):
                chunks.append(mk_group(gi, kind, c0, w))
                if gi == 2:
                    chunks += [mk_convT(0, 8), mk_convT(8, 12), mk_conv(0), mk_conv(1), mk_conv(2)]
            return chunks

        def phase2(i, tt):
            T, first, last = tt["T"], tt["first"], tt["last"]
            ssm_dst, init_ssm, init_conv = tt["ssm_dst"], tt["init_ssm"], tt["init_conv"]
            x_src, y_dst = tt["x_src"], tt["y_dst"]
            p = i % 2
            F1t, F1B = F1b[p], B["F1b%d" % p]
            F3t, F3B = F3b[p], B["F3b%d" % p]
            H2t, H2B = H2[p], B["H2_%d" % p]
            actT, actB = actT2[p], B["actT%d" % p]
            dtc = 16 if p == 0 else 176
            dtB = B["dt%d" % p]
            dt_ap = sm[:T, dtc:dtc + 16]
            rawv = v3(raw[:, 0:12 * (3 + T)], 3 + T)
            actv = v3(actT[:, 0:12 * T], T)
            acum = sm[:T, 64:80]
            A4 = sm[:T, 64:128]
            Ms = [(H3a, B["H3a"]), (H3b, B["H3b"])]
            chunks = []

            def d_init():
                if not first:
                    return
                if init_ssm is None:
                    dve(lambda: nc.vector.memset(hT[:], 0.0), [], [B["hT"]])
                    dve(lambda: nc.vector.memset(hTb[:], 0.0), [], [B["hTb"]])
                else:
                    h3a32 = H3a[:, :].bitcast(F32)
                    h3b32 = H3b[:, :].bitcast(F32)
                    st3 = init_ssm.rearrange("(j p) n -> p j n", p=128)
                    S.dma("ldst", h3a32.rearrange("p (j n) -> p j n", n=128), st3[:, 0:4, :], [], [B["H3a"]])
                    S.dma("ldst2", h3b32.rearrange("p (j n) -> p j n", n=128), st3[:, 4:8, :], [], [B["H3b"]])
                    for j in range(8):
                        srcj, sBj = (h3a32, B["H3a"]) if j < 4 else (h3b32, B["H3b"])
                        jj = j % 4
                        pe(lambda j=j, srcj=srcj, jj=jj: nc.tensor.matmul(P_X[:, j * 128:(j + 1) * 128],
                                                                          lhsT=srcj[:, jj * 128:(jj + 1) * 128],
                                                                          rhs=identF[:], start=True, stop=True),
                           [sBj, cB], [B["PX"]], signal=(j == 7))
                    dve(lambda: nc.vector.tensor_copy(out=hT[:], in_=P_X[:]), [B["PX"]], [B["hT"]])
                    act(lambda: nc.scalar.copy(out=hTb[:], in_=hT[:]), [B["hT"]], [B["hTb"]])
            chunks.append(d_init)

            def d_gmlp():
                for h in range(8):
                    pe(lambda h=h: nc.tensor.matmul(P_X[:T, h * 128:(h + 1) * 128], lhsT=WmT[:T, h, :T],
                                                    rhs=H2t[:T, h * 128:(h + 1) * 128], start=True, stop=True),
                       [H2B, cB], [B["PX"]], signal=(h == 7), n=140)
                for h in range(8):
                    dve(lambda h=h: nc.vector.scalar_tensor_tensor(out=H1[:T, h * 128:(h + 1) * 128],
                                                                   in0=P_X[:T, h * 128:(h + 1) * 128], scalar=bsT[:T, h:h + 1],
                                                                   in1=F1t[:T, h * 128:(h + 1) * 128], op0=ALU.add, op1=ALU.mult),
                        [B["PX"], F1B, cB], [B["H1"]], n=128)
                for k in range(8):
                    pe(lambda k=k: nc.tensor.transpose(out=P_T[:, k * T:(k + 1) * T], in_=H1[:T, k * 128:(k + 1) * 128],
                                                       identity=identB[:T, :T]), [B["H1"], cB], [B["PT"]], signal=(k == 7), n=T)
                act(lambda: nc.scalar.copy(out=aT[:, 0:8 * T], in_=P_T[:, 0:8 * T]), [B["PT"]], [B["aT"]])

            def d_acum():
                dve(lambda: nc.vector.tensor_tensor(out=sm[:T, 32:48], in0=dt_ap, in1=a_bc[:T, :], op=ALU.mult),
                    [dtB, cB], [B["A4"]])
                pe(lambda: nc.tensor.matmul(P_S[:T, 0:16], lhsT=triF[:T, :T], rhs=sm[:T, 32:48], start=True, stop=True),
                   [B["A4"], cB], [B["PS"]])
                dve(lambda: nc.vector.tensor_copy(out=v3(sm[:T, 64:96], 16), in_=P_S[:T, 0:16].unsqueeze(1).to_broadcast([T, 2, 16])),
                    [B["PS"]], [B["A4"]])
                dve(lambda: nc.vector.tensor_scalar(out=v3(sm[:T, 96:128], 16), in0=P_S[:T, 0:16].unsqueeze(1).to_broadcast([T, 2, 16]),
                                                    scalar1=-1.0, scalar2=None, op0=ALU.mult), [B["PS"]], [B["A4"]])
                pe(lambda: nc.tensor.matmul(P_S[0:64, 128:128 + T], lhsT=A4, rhs=identF[:T, :T], start=True, stop=True),
                   [B["A4"], cB], [B["PS"]])
                ATp = P_S[0:64, 128:128 + T]
                hi_b = smb[:, 0:T]
                lo_b = smb[:, 128:128 + T]
                Vb = smb[:, 256:256 + T]
                hi_f = sm[0:64, 256:256 + T]
                dve(lambda: nc.vector.tensor_copy(out=hi_b, in_=ATp), [B["PS"]], [B["V"]])
                dve(lambda: nc.vector.tensor_copy(out=hi_f, in_=hi_b), [B["V"]], [B["V"]])
                dve(lambda: nc.vector.tensor_tensor(out=hi_f, in0=ATp, in1=hi_f, op=ALU.subtract), [B["PS"], B["V"]], [B["V"]])
                dve(lambda: nc.vector.tensor_scalar(out=lo_b, in0=hi_f, scalar1=ml, scalar2=None, op0=ALU.mult),
                    [B["V"], cB], [B["V"]])
                dve(lambda: nc.vector.scalar_tensor_tensor(out=Vb, in0=hi_b, scalar=mh, in1=lo_b, op0=ALU.mult, op1=ALU.add),
                    [B["V"], cB], [B["V"]])
                Rv = v3(Rfl[:, 0:16 * T], T)
                dve(lambda: nc.vector.tensor_tensor(out=Rv[0:32], in0=Vb[0:32].unsqueeze(1).to_broadcast([32, 16, T]),
                                                    in1=RmB[0:32, :].unsqueeze(2).to_broadcast([32, 16, T]), op=ALU.mult),
                    [B["V"], cB], [B["Rfl"]])
                if rstate.get("T") != T:
                    rstate["T"] = T
                    dve(lambda: nc.vector.tensor_copy(out=Rv[32:64], in_=RmB[32:64, :].unsqueeze(2).to_broadcast([32, 16, T])),
                        [cB], [B["Rfl"]])
                dve(lambda: nc.vector.tensor_copy(out=lhsT_d[32:64, 0:T], in_=Vb[32:64]), [B["V"]], [B["lhsT_d"]])
                pe(lambda: nc.tensor.matmul(P_S[:, 16:32], lhsT=onesF[:T, :], rhs=sm[:T, 32:48], start=True, stop=True),
                   [B["A4"], cB], [B["PS"]])
                dve(lambda: nc.vector.tensor_tensor(out=sm[:T, 144:160], in0=P_S[:T, 16:32], in1=acum, op=ALU.subtract),
                    [B["PS"], B["A4"]], [B["dte"]])
                dve(lambda: nc.vector.tensor_copy(out=sm[:, 160:176], in_=P_S[:, 16:32]), [B["PS"]], [B["cd"]])
                act(lambda: nc.scalar.activation(out=sm[:T, 144:160], in_=sm[:T, 144:160], func=AF.Exp), [B["dte"]], [B["dte"]], tab="exp")
                act(lambda: nc.scalar.activation(out=sm[:, 160:176], in_=sm[:, 160:176], func=AF.Exp), [B["cd"]], [B["cd"]], tab="exp")
                act(lambda: nc.scalar.activation(out=sm[:T, 128:144], in_=acum, func=AF.Exp), [B["A4"]], [B["Ee"]], tab="exp")

            def d_cb():
                for g in range(2):
                    pe(lambda g=g: nc.tensor.matmul(P_S[:T, 256 + g * T:256 + (g + 1) * T], lhsT=actv[:, 8 + g, :],
                                                    rhs=actv[:, 10 + g, :], start=True, stop=True),
                       [actB], [B["PS"]], signal=(g == 1), n=T)
                dve(lambda: nc.vector.tensor_tensor(out=v3(CBm[:T, 0:2 * T], T), in0=v3(P_S[:T, 256:256 + 2 * T], T),
                                                    in1=triF[:T, :T].unsqueeze(1).to_broadcast([T, 2, T]), op=ALU.mult),
                    [B["PS"], cB], [B["CBm"]])
                for j in range(8):
                    pe(lambda j=j: nc.tensor.transpose(out=P_T[:T, j * 128:(j + 1) * 128], in_=actv[:, j, :], identity=identB[:]),
                       [actB, cB], [B["PT"]], signal=(j == 7), n=140)
                dve(lambda: nc.vector.tensor_tensor(out=v3(H4[:T, :], 64), in0=v3(P_T[:T, :], 64),
                                                    in1=dt_ap.unsqueeze(2).to_broadcast([T, 16, 64]), op=ALU.mult),
                    [B["PT"], dtB], [B["H4"]])
                for g in range(2):
                    pe(lambda g=g: nc.tensor.transpose(out=P_T[:T, g * 128:(g + 1) * 128], in_=actv[:, 8 + g, :],
                                                       identity=identB[:]), [actB, cB], [B["PT"]], signal=(g == 1), n=140)
                act(lambda: nc.scalar.copy(out=Btm[:T, :], in_=P_T[:T, 0:256]), [B["PT"]], [B["Btm"]], n=256)

            def mk_decay(g):
                def f():
                    Pd, PdB = (P_Y, B["PY"]) if g == 0 else (P_X, B["PX"])
                    ncol = 8 * T
                    negv = v3(NEG4[:T, 0:512], 128)[:, :, 0:T] if T != 128 else NEG4[:T, 0:512]
                    Mt, MB = Ms[g]
                    for hc in range(2):
                        o3 = v3(Pd[:T, hc * 4 * T:(hc + 1) * 4 * T], T) if T != 128 else Pd[:T, hc * 512:(hc + 1) * 512]
                        r0 = (g * 8 + hc * 4) * T
                        pe(lambda o3=o3, r0=r0: nc.tensor.matmul(o3, lhsT=lhsT_d[:, 0:T],
                                                                 rhs=(v3(Rfl[:, r0:r0 + 4 * T], T) if T != 128 else Rfl[:, r0:r0 + 512]),
                                                                 start=True, stop=False),
                           [B["lhsT_d"], B["Rfl"], cB], [PdB], signal=False)
                        pe(lambda o3=o3: nc.tensor.matmul(o3, lhsT=identB[:T, :T], rhs=negv, start=False, stop=True),
                           [cB], [PdB], signal=(hc == 1))
                    act(lambda: nc.scalar.activation(out=Mt[:T, 0:ncol], in_=Pd[:T, 0:ncol], func=AF.Exp), [PdB], [MB], tab="exp")
                    dve(lambda: nc.vector.tensor_tensor(out=v3(Mt[:T, 0:ncol], T), in0=v3(Mt[:T, 0:ncol], T),
                                                        in1=CBm[:T, g * T:(g + 1) * T].unsqueeze(1).to_broadcast([T, 8, T]),
                                                        op=ALU.mult), [MB, B["CBm"]], [MB], n=600)
                return f

            def mk_ydiag(b):
                def f():
                    Mt, MB = Ms[b]
                    for jj in range(4):
                        j = 4 * b + jj
                        pe(lambda j=j, jj=jj: nc.tensor.matmul(P_Y[:T, j * 128:(j + 1) * 128], lhsT=actv[:, j, :],
                                                               rhs=diagD[:, j, :], start=(jj == 0), stop=False,
                                                               skip_group_check=True),
                           [actB, cB], [B["PY"]], signal=False, n=200)
                    for hh in range(8):
                        h = 8 * b + hh
                        pe(lambda h=h, hh=hh: nc.tensor.matmul(P_Y[:T, h * 64:(h + 1) * 64], lhsT=Mt[:T, hh * T:(hh + 1) * T],
                                                               rhs=H4[:T, h * 64:(h + 1) * 64], start=False, stop=(hh == 7),
                                                               skip_group_check=True),
                           [MB, B["H4"]], [B["PY"]], signal=(hh == 7), n=200)
                return f

            def d_yoff():
                for g in range(2):
                    pe(lambda g=g: nc.tensor.matmul(P_X[:T, g * 512:(g + 1) * 512], lhsT=actv[:, 10 + g, :],
                                                    rhs=hTb[:, g * 512:(g + 1) * 512], start=True, stop=True),
                       [actB, B["hTb"]], [B["PX"]], signal=(g == 1))
                dve(lambda: nc.vector.tensor_tensor(out=v3(Ft[:T, :], 64), in0=v3(P_X[:T, :], 64),
                                                    in1=sm[:T, 128:144].unsqueeze(2).to_broadcast([T, 16, 64]), op=ALU.mult),
                    [B["PX"], B["Ee"]], [B["Ft"]])
                dve(lambda: nc.vector.tensor_tensor(out=Ft[:T, :], in0=P_Y[:T, :], in1=Ft[:T, :], op=ALU.add),
                    [B["PY"], B["Ft"]], [B["Ft"]])
                dve(lambda: nc.vector.tensor_tensor(out=Ft[:T, :], in0=Ft[:T, :], in1=F3t[:T, :], op=ALU.mult),
                    [B["Ft"], F3B], [B["Ft"]])
                act(lambda: nc.scalar.activation(out=H1[:T, :], in_=Ft[:T, :], func=AF.Square, accum_out=sm[:T, 10:11]),
                    [B["Ft"]], [B["H1"], B["ss2"]])
                rstd_from_ss(T, 10, B["ss2"], B["rs2"])
                dve(lambda: nc.vector.tensor_scalar(out=H1[:T, :], in0=Ft[:T, :], scalar1=sm[:T, 11:12], scalar2=None,
                                                    op0=ALU.mult), [B["Ft"], B["rs2"]], [B["H1"]], n=600)
                S.dma("ld_xr", Ft[:T, :], x_src, [], [B["Ft"]])
                for k in range(8):
                    pe(lambda k=k: nc.tensor.transpose(out=P_T[:, k * T:(k + 1) * T], in_=H1[:T, k * 128:(k + 1) * 128],
                                                       identity=identB[:T, :T]), [B["H1"], cB], [B["PT"]], signal=(k == 7), n=T)
                act(lambda: nc.scalar.copy(out=bT[:, 0:8 * T], in_=P_T[:, 0:8 * T]), [B["PT"]], [B["bT"]])

            def d_state():
                pool(lambda: nc.gpsimd.tensor_tensor(out=v3(H4[:T, :], 64), in0=v3(H4[:T, :], 64),
                                                     in1=sm[:T, 144:160].unsqueeze(2).to_broadcast([T, 16, 64]), op=ALU.mult),
                     [B["H4"], B["dte"]], [B["H4"]])
                for g in range(2):
                    pe(lambda g=g: nc.tensor.matmul(P_Y[:, g * 512:(g + 1) * 512], lhsT=Btm[:T, g * 128:(g + 1) * 128],
                                                    rhs=H4[:T, g * 512:(g + 1) * 512], start=True, stop=True),
                       [B["Btm"], B["H4"]], [B["PY"]], signal=(g == 1))
                pool(lambda: nc.gpsimd.tensor_tensor(out=v3(hT[:, :], 64), in0=v3(hT[:, :], 64),
                                                     in1=sm[:, 160:176].unsqueeze(2).to_broadcast([128, 16, 64]), op=ALU.mult),
                     [B["hT"], B["cd"]], [B["hT"]])
                dve(lambda: nc.vector.tensor_tensor(out=hT[:, :], in0=P_Y[:, :], in1=hT[:, :], op=ALU.add),
                    [B["PY"], B["hT"]], [B["hT"]])
                if not last:
                    act(lambda: nc.scalar.copy(out=hTb[:, :], in_=hT[:, :]), [B["hT"]], [B["hTb"]])

            def d_out():
                for part, (src, sB_) in enumerate(((aT, B["aT"]), (bT, B["bT"]))):
                    for cg in range(2):
                        for kk in range(8):
                            k = part * 8 + kk
                            pe(lambda cg=cg, k=k, src=src, kk=kk: nc.tensor.matmul(P_X[:T, cg * 512:(cg + 1) * 512],
                                                                                   lhsT=src[:, kk * T:(kk + 1) * T],
                                                                                   rhs=Wout[:, k, cg * 512:(cg + 1) * 512],
                                                                                   start=(k == 0), stop=(k == 15),
                                                                                   skip_group_check=True),
                               [sB_, B["WoA"], B["WoD"]], [B["PX"]], signal=(kk == 7 and cg == 1))
                dve(lambda: nc.vector.tensor_tensor(out=Ft[:T, :], in0=P_X[:T, :], in1=Ft[:T, :], op=ALU.add),
                    [B["PX"], B["Ft"]], [B["Ft"]])
                act(lambda: nc.scalar.activation(out=H1[:T, :], in_=Ft[:T, :], func=AF.Square, accum_out=sm[:T, 10:11]),
                    [B["Ft"]], [B["H1"], B["ss2"]])
                rstd_from_ss(T, 10, B["ss2"], B["rs2"])
                dve(lambda: nc.vector.scalar_tensor_tensor(out=Ft[:T, :], in0=Ft[:T, :], scalar=sm[:T, 11:12], in1=nf_bc[:T, :],
                                                           op0=ALU.mult, op1=ALU.mult), [B["Ft"], B["rs2"], cB], [B["Ft"]])
                S.dma("st_y", y_dst, Ft[:T, :], [B["Ft"]], [])
                if last:
                    for j in range(8):
                        pe(lambda j=j: nc.tensor.matmul(P_Y[:, j * 128:(j + 1) * 128], lhsT=hT[:, j * 128:(j + 1) * 128],
                                                        rhs=identF[:], start=True, stop=True),
                           [B["hT"], cB], [B["PY"]], signal=(j == 7), n=512)
                    st3o = ssm_dst.rearrange("(j p) n -> p j n", p=128)
                    for hf, (stg, stB) in enumerate(((aT, B["aT"]), (bT, B["bT"]))):
                        s32 = stg[:, :].bitcast(F32)
                        dve(lambda hf=hf, s32=s32: nc.vector.tensor_copy(out=s32[:, 0:512], in_=P_Y[:, hf * 512:(hf + 1) * 512]),
                            [B["PY"]], [stB], n=512)
                        S.dma("st_ssm%d" % hf, st3o[:, hf * 4:(hf + 1) * 4, :], s32[:, 0:512].rearrange("p (j n) -> p j n", n=128),
                              [stB], [])

            chunks += [d_acum, d_gmlp, d_cb, mk_decay(0), mk_decay(1), mk_ydiag(0), mk_ydiag(1), d_yoff, d_state, d_out]
            return chunks

        tiles = []
        for t in range(SEQ // 128):
            tiles.append(dict(T=128, x_src=xp[t * 128:(t + 1) * 128, :], y_dst=yp[t * 128:(t + 1) * 128, :],
                              first=(t == 0), last=(t == SEQ // 128 - 1), conv_dst=oconv_p, ssm_dst=ossm_p, v_dst=None,
                              init_ssm=None, init_conv=None))
        for b in range(NS):
            tiles.append(dict(T=TS, x_src=xs[b], y_dst=ys[b], first=True, last=True, conv_dst=oconv_s[b], ssm_dst=ossm_s[b],
                              v_dst=ov_s[b], init_ssm=sssm[b], init_conv=sconv[b]))

        def mk_load(idx):
            def f():
                if idx < len(tiles):
                    tt = tiles[idx]
                    S.dma("ld_x", XF[:tt["T"], :], tt["x_src"], [], [B["XF"]])
            return f

        mk_load(0)()
        for c in phase1(0, tiles[0], mk_load(1)):
            c()
        for i in range(len(tiles)):
            c2 = phase2(i, tiles[i])
            c1 = phase1(i + 1, tiles[i + 1], mk_load(i + 2)) if i + 1 < len(tiles) else []
            n1, n2 = len(c1), len(c2)
            a = b_ = 0
            while a < n2 or b_ < n1:
                if b_ >= n1 or (a < n2 and a * max(n1, 1) <= b_ * n2):
                    c2[a]()
                    a += 1
                else:
                    c1[b_]()
                    b_ += 1

        S.flush()
        for key in list(S.sem.keys()):
            if key.startswith("st_"):
                S._wait("sp", key, S.cnt[key])
    return nc


_NC_CACHE = {}


def kernel(x_prompt, x_sample, state_ssm, state_conv, norm_in_w, w_in, gmlp_ln_w, gmlp_ln_b, gmlp_ws, gmlp_bs,
           conv_w, conv_b, dt_bias, a_log, d_skip, ssm_norm_w, w_out, norm_f_w):
    f = lambda a: np.ascontiguousarray(np.asarray(a, dtype=np.float32))
    n = 8
    if "nc" not in _NC_CACHE:
        _NC_CACHE["nc"] = build_nc()
    nc = _NC_CACHE["nc"]
    shared = {
        "norm_in_w": f(norm_in_w[0]), "w_in": f(w_in[0]), "gmlp_ln_w": f(gmlp_ln_w[0]), "gmlp_ln_b": f(gmlp_ln_b[0]),
        "gmlp_ws": f(gmlp_ws[0]), "gmlp_bs": f(gmlp_bs[0]), "conv_w": f(conv_w[0]), "conv_b": f(conv_b[0]),
        "dt_bias": f(dt_bias[0]), "a_log": f(a_log[0]), "d_skip": f(d_skip[0]), "ssm_norm_w": f(ssm_norm_w[0]),
        "w_out": f(w_out[0]), "norm_f_w": f(norm_f_w),
    }
    in_maps = []
    for c in range(n):
        m = dict(shared)
        m["xp"] = f(x_prompt[c])
        m["xs"] = f(x_sample[NS * c:NS * (c + 1)])
        m["sssm"] = f(np.asarray(state_ssm)[0, NS * c:NS * (c + 1)].reshape(NS, 1024, 128))
        m["sconv"] = f(np.asarray(state_conv)[0, NS * c:NS * (c + 1)])
        in_maps.append(m)
    res = run_bass_kernel_spmd(nc, in_maps, core_ids=list(range(n)))
    R = res.results
    y_prompt = np.stack([R[c]["yp"] for c in range(n)], 0)
    y_sample = np.concatenate([R[c]["ys"] for c in range(n)], 0)
    ssm_p = np.stack([R[c]["ossm_p"].reshape(16, 64, 128) for c in range(n)], 0)[None]
    conv_p = np.stack([R[c]["oconv_p"] for c in range(n)], 0)[None]
    ssm_s = np.concatenate([R[c]["ossm_s"].reshape(NS, 16, 64, 128) for c in range(n)], 0)[None]
    conv_s = np.concatenate([R[c]["oconv_s"] for c in range(n)], 0)[None]
    v_s = np.concatenate([R[c]["ov_s"] for c in range(n)], 0)[None]
    return (y_prompt.astype(np.float32), y_sample.astype(np.float32), ssm_p.astype(np.float32), conv_p.astype(np.float32),
            ssm_s.astype(np.float32), conv_s.astype(np.float32), v_s.astype(np.float32))
```

```python
import contextlib
import numpy as np
import concourse.bass as bass
import concourse.mybir as mybir
from concourse.bass_utils import run_bass_kernel_spmd

F32 = mybir.dt.float32
BF16 = mybir.dt.bfloat16
AF = mybir.ActivationFunctionType
ALU = mybir.AluOpType

D = 1024
SEQ = 4096
NS = 4
TS = 64
DIN = 5648
CONV = 1536
EPS = 1e-5
OU, OV, OG, OZ, OX, ODT = 0, 1024, 2048, 3072, 4096, 5632


class Buf:
    def __init__(self, name, excl=False):
        self.name = name
        self.w = set()
        self.r = set()
        self.xr = set()
        self.excl = excl


class Unit:
    __slots__ = ("eng", "fns", "reads", "writes", "cost", "deps", "key", "idx", "lat", "tab", "line")

    def __init__(self, eng, key=None):
        self.eng = eng
        self.fns = []
        self.reads = []
        self.writes = []
        self.cost = 0.0
        self.deps = set()
        self.key = key
        self.lat = 0.0
        self.tab = None
        self.line = 0


class Sched:
    XLAT = 0.8
    COST = {"pe": (0.015, 1.0 / 2300), "act": (0.2, 1.0 / 1200), "dve": (0.1, 1.0 / 920), "pool": (0.1, 1.0 / 480)}

    def __init__(self, nc, es):
        self.nc = nc
        self.es = es
        self.engs = {"pe": nc.tensor, "act": nc.scalar, "dve": nc.vector, "pool": nc.gpsimd, "sp": nc.sync}
        self.sem = {}
        self.cnt = {}
        self.waited = {e: {} for e in self.engs}
        for e in ("pe", "act", "dve", "pool"):
            self.newsem(e)
        self.units = []
        self.pending = {}
        self.last_dma = {}
        self.sigval = {}

    def newsem(self, key):
        self.sem[key] = self.es.enter_context(self.nc.semaphore("s_" + key))
        self.cnt[key] = 0

    def _close(self, u):
        deps = set()
        for b in u.reads:
            deps |= b.w
            if b.excl:
                deps |= {x for x in b.xr if self.units[x].eng != u.eng}
        for b in u.writes:
            deps |= b.w
            deps |= b.r
            deps |= b.xr
        u.idx = len(self.units)
        u.deps = deps
        for b in u.reads:
            if b.excl:
                b.xr.add(u.idx)
            else:
                b.r.add(u.idx)
        for b in u.writes:
            b.w = {u.idx}
            b.r = set()
            b.xr = set()
        self.units.append(u)

    def op(self, e, fn, reads=(), writes=(), signal=True, n=None, tab=None):
        u = self.pending.get(e)
        if u is None:
            u = Unit(e)
            self.pending[e] = u
        u.fns.append(fn)
        if not u.line:
            import sys as _sys
            u.line = _sys._getframe(2).f_lineno
        if tab is not None:
            u.tab = tab
        u.reads += list(reads)
        u.writes += list(writes)
        c0, c1 = self.COST[e]
        u.cost += c0 + c1 * (n if n is not None else (512 if e == "pe" else 1024))
        if signal:
            del self.pending[e]
            self._close(u)

    def dma(self, key, out, in_, reads=(), writes=(), **kw):
        if key not in self.sem:
            self.newsem(key)
        u = Unit("sp", key)
        import sys as _sys
        u.line = _sys._getframe(1).f_lineno
        u.fns.append(lambda: self.nc.sync.dma_start(out=out, in_=in_, **kw))
        u.reads = list(reads)
        u.writes = list(writes)
        u.cost = 0.15
        u.lat = 4.0
        self._close(u)
        if key in self.last_dma:
            u.deps.add(self.last_dma[key])
        self.last_dma[key] = u.idx

    def _wait(self, e, key, val):
        if val <= 0 or self.waited[e].get(key, 0) >= val:
            return
        self.engs[e].wait_ge(self.sem[key], val)
        self.waited[e][key] = val

    def flush(self):
        assert not self.pending, list(self.pending)
        units = self.units
        n = len(units)
        XL, SL, TSW, EPS_T = self.XLAT, 0.2, 0.0, 1.25
        users = [[] for _ in range(n)]
        for u in units:
            for d in u.deps:
                users[d].append(u.idx)
        bl = [0.0] * n
        for i in range(n - 1, -1, -1):
            u = units[i]
            m = 0.0
            for v in users[i]:
                c = bl[v] + (SL if units[v].eng == u.eng else XL)
                if c > m:
                    m = c
            bl[i] = u.cost + u.lat + m
        ndep = [len(u.deps) for u in units]
        rdy = [0.0] * n
        fin = [0.0] * n
        free = {e: 0.0 for e in self.engs}
        rel = {e: set() for e in self.engs}
        for i in range(n):
            if ndep[i] == 0:
                rel[units[i].eng].add(i)
        cur_tab = None
        order = []
        nleft = n
        while nleft:
            best_e, best_t = None, None
            for e, ss in rel.items():
                if not ss:
                    continue
                te = max(free[e], min(rdy[i] for i in ss))
                if best_t is None or te < best_t:
                    best_e, best_t = e, te
            e = best_e
            cand = [i for i in rel[e] if rdy[i] <= best_t + EPS_T]
            if e == "act" and cur_tab is not None:
                same = [i for i in cand if units[i].tab in (None, cur_tab)]
                if same:
                    cand = same
            i = max(cand, key=lambda j: (bl[j], -j))
            u = units[i]
            rel[e].remove(i)
            st = max(free[e], rdy[i])
            c = u.cost
            if u.tab is not None:
                if cur_tab is not None and u.tab != cur_tab:
                    c += TSW
                cur_tab = u.tab
            free[e] = st + c
            fin[i] = st + c + u.lat
            order.append(i)
            nleft -= 1
            for v in users[i]:
                t = fin[i] + (SL if units[v].eng == e else XL)
                if t > rdy[v]:
                    rdy[v] = t
                ndep[v] -= 1
                if ndep[v] == 0:
                    rel[units[v].eng].add(v)
        assert len(order) == n
        self.est_total = max(fin) if fin else 0.0
        for i in order:
            u = units[i]
            e = u.eng
            for d in sorted(u.deps):
                if units[d].eng == e and e == "pe":
                    continue
                key, val = self.sigval[d]
                self._wait(e, key, val)
            if e == "sp":
                ins = u.fns[0]()
                self.cnt[u.key] += 16
                ins.then_inc(self.sem[u.key], 16)
                self.sigval[i] = (u.key, self.cnt[u.key])
            else:
                ins = None
                for f in u.fns:
                    ins = f()
                self.cnt[e] += 1
                ins.then_inc(self.sem[e], 1)
                self.sigval[i] = (e, self.cnt[e])
        self.units = []


def build_nc(SEQ=SEQ, NS=NS):
    nc = bass.Bass("TRN2", target_bir_lowering=False)

    def din(name, shape):
        return nc.dram_tensor(name, list(shape), F32, kind="ExternalInput").ap()

    def dout(name, shape):
        return nc.dram_tensor(name, list(shape), F32, kind="ExternalOutput").ap()

    xp = din("xp", (SEQ, D))
    xs = din("xs", (NS, TS, D))
    sssm = din("sssm", (NS, 1024, 128))
    sconv = din("sconv", (NS, 3, CONV))
    nin = din("norm_in_w", (D,))
    w_in = din("w_in", (D, DIN))
    lnw = din("gmlp_ln_w", (D,))
    lnb = din("gmlp_ln_b", (D,))
    gws = din("gmlp_ws", (8, 128, 128))
    gbs = din("gmlp_bs", (8, 128))
    cvw = din("conv_w", (4, CONV))
    cvb = din("conv_b", (CONV,))
    dtb = din("dt_bias", (16,))
    alog = din("a_log", (16,))
    dsk = din("d_skip", (16,))
    snw = din("ssm_norm_w", (D,))
    w_out = din("w_out", (2048, D))
    nfw = din("norm_f_w", (D,))

    yp = dout("yp", (SEQ, D))
    ys = dout("ys", (NS, TS, D))
    ossm_p = dout("ossm_p", (1024, 128))
    oconv_p = dout("oconv_p", (3, CONV))
    ossm_s = dout("ossm_s", (NS, 1024, 128))
    oconv_s = dout("oconv_s", (NS, 3, CONV))
    ov_s = dout("ov_s", (NS, TS, D))

    es = contextlib.ExitStack()
    with es:
        S = Sched(nc, es)

        def sb(name, shape, dt):
            return es.enter_context(nc.sbuf_tensor(name, list(shape), dt))

        def ps(name, shape, dt):
            return es.enter_context(nc.psum_tensor(name, list(shape), dt))

        Win = sb("Win", (128, 8, DIN), BF16)
        Wout = sb("Wout", (128, 16, D), BF16)
        identF = sb("identF", (128, 128), F32)
        identB = sb("identB", (128, 128), BF16)
        triF = sb("triF", (128, 128), F32)
        onesF = sb("onesF", (128, 128), F32)
        NEG4 = sb("NEG4", (128, 512), BF16)
        WmT = sb("WmT", (128, 8, 128), BF16)
        lnw_bc = sb("lnw_bc", (128, D), F32)
        lnb_bc = sb("lnb_bc", (128, D), F32)
        nf_bc = sb("nf_bc", (128, D), F32)
        diagD = sb("diagD", (128, 8, 128), BF16)
        cst = sb("cst", (128, 256), F32)
        cwb = cst[:, 64:124].rearrange("p (b k) -> p b k", k=5)
        RmB = sb("RmB", (64, 16), BF16)
        lhsT_d = sb("lhsT_d", (128, 128), BF16)
        XF = sb("XF", (128, D), F32)
        Ft = sb("Ft", (128, D), F32)
        hT = sb("hT", (128, D), F32)
        H1x = sb("H1x", (128, D), BF16)
        xT = sb("xT", (128, D), BF16)
        F1b = [sb("F1b%d" % i, (128, D), BF16) for i in range(2)]
        F3b = [sb("F3b%d" % i, (128, D), BF16) for i in range(2)]
        H2 = [sb("H2_%d" % i, (128, D), BF16) for i in range(2)]
        xbc_bf = sb("xbc_bf", (128, CONV), BF16)
        H1 = sb("H1", (128, D), BF16)
        H3a = sb("H3a", (128, D), BF16)
        H3b = sb("H3b", (128, D), BF16)
        H4 = sb("H4", (128, D), BF16)
        aT = sb("aT", (128, D), BF16)
        bT = sb("bT", (128, D), BF16)
        hTb = sb("hTb", (128, D), BF16)
        raw = sb("raw", (128, 12 * 131), BF16)
        cvA = sb("cvA", (128, 512), F32)
        cvB = sb("cvB", (128, 512), F32)
        cvD = sb("cvD", (128, 512), F32)
        actT2 = [sb("actT%d" % i, (128, 12 * 128), BF16) for i in range(2)]
        Rfl = sb("Rfl", (128, 2048), BF16)
        CBm = sb("CBm", (128, 256), BF16)
        Btm = sb("Btm", (128, 256), BF16)
        sm = sb("sm", (128, 384), F32)
        smb = sb("smb", (64, 384), BF16)

        P_IP = ps("P_IP", (128, 1024), F32)
        P_X = ps("P_X", (128, 1024), F32)
        P_Y = ps("P_Y", (128, 1024), F32)
        P_T = ps("P_T", (128, 1024), BF16)
        P_S = ps("P_S", (128, 512), F32)

        B = {n: Buf(n) for n in (
            "WinA", "WinD", "const", "XF", "Ft", "hT", "H1x", "xT", "F1b0", "F1b1", "F3b0", "F3b1", "H2_0", "H2_1",
            "xbc", "actT0", "actT1", "H1", "H3a", "H3b", "H4", "aT", "bT", "hTb", "raw", "cvA", "cvB", "cvD0", "cvD1", "cvD2", "cvD3", "Rfl", "CBm", "Btm",
            "ip0", "ip1", "PX", "PY", "PT", "PS",
            "ss1", "rs1", "ss2", "rs2", "vst", "dt0", "dt1", "A4", "Ee", "dte", "cd", "V", "lhsT_d")}
        for n in ("ip0", "ip1", "PX", "PY", "PT", "PS"):
            B[n].excl = True

        S.newsem("const")
        nconst = [0]

        def cdma(out, in_, **kw):
            nc.sync.dma_start(out=out, in_=in_, **kw).then_inc(S.sem["const"], 16)
            nconst[0] += 16

        cdma(Ft[0:8, 768:896], gbs[:, :])
        cdma(Ft[8:16, 768:896], nin.rearrange("(k p) -> k p", p=128))
        cdma(Ft[16:24, 768:896], snw.rearrange("(k p) -> k p", p=128))
        cdma(Ft[0:4, 0:768], cvw[:, 0:768])
        cdma(hT[0:4, 0:768], cvw[:, 768:CONV])
        cdma(Ft[4:5, 0:768], cvb[0:768].rearrange("(o n) -> o n", o=1))
        cdma(hT[4:5, 0:768], cvb[768:CONV].rearrange("(o n) -> o n", o=1))
        cdma(lnw_bc[:], lnw.partition_broadcast(128))
        cdma(lnb_bc[:], lnb.partition_broadcast(128))
        cdma(nf_bc[:], nfw.partition_broadcast(128))
        cdma(cst[:, 32:48], alog.partition_broadcast(128))
        cdma(cst[:, 48:64], dtb.partition_broadcast(128))
        dsk2 = dsk.rearrange("(j t) -> t j", t=2)
        cdma(cst[0:64, 24:32], dsk2[0].partition_broadcast(64), allow_slow_non_contiguous=True)
        cdma(cst[64:128, 24:32], dsk2[1].partition_broadcast(64), allow_slow_non_contiguous=True)
        for h in range(8):
            cdma(XF[:, h * 128:(h + 1) * 128], gws[h])
        for e in ("pe", "act", "dve", "pool"):
            S.engs[e].wait_ge(S.sem["const"], nconst[0])
            S.waited[e]["const"] = nconst[0]
        S.waited["sp"]["const"] = 0
        S.cnt["const"] = nconst[0]

        cB = B["const"]
        def pool(fn, reads=(), writes=(), n=None):
            return S.op("pool", fn, reads, writes, True, n)

        def dve(fn, reads=(), writes=(), n=None):
            return S.op("dve", fn, reads, writes, True, n)

        def act(fn, reads=(), writes=(), n=None, tab=None):
            return S.op("act", fn, reads, writes, True, n, tab)

        def pe(fn, reads=(), writes=(), signal=True, n=None):
            return S.op("pe", fn, reads, writes, signal, n)

        def asel(t, pattern, cmp, base, cm):
            pool(lambda: nc.gpsimd.affine_select(out=t, in_=t, pattern=pattern, compare_op=cmp, fill=0.0,
                                                 base=base, channel_multiplier=cm), [cB], [cB])

        for t in (identF, triF, onesF):
            pool(lambda t=t: nc.gpsimd.memset(t[:], 1.0), [], [cB])
        asel(identF[:], [[-1, 128]], ALU.is_equal, 0, 1)
        B["cNin"] = Buf("cNin")
        pe(lambda: nc.tensor.matmul(P_S[:, 0:24], lhsT=Ft[0:24, 768:896], rhs=identF[0:24, 0:24], start=True, stop=True),
           [cB], [B["PS"]])
        dve(lambda: nc.vector.tensor_copy(out=cst[:, 0:24], in_=P_S[:, 0:24]), [B["PS"]], [cB, B["cNin"]])
        bsT = cst[:, 0:8]
        nin_pp = cst[:, 8:16]
        snw_pp = cst[:, 16:24]
        asel(triF[:], [[1, 128]], ALU.is_ge, 0, -1)
        RmF = cst[0:64, 128:144]
        tmpm = cst[0:64, 144:160]
        pool(lambda: nc.gpsimd.memset(RmF, 0.0), [], [cB])
        for q in range(4):
            pool(lambda: nc.gpsimd.memset(tmpm, 1.0), [], [cB])
            asel(tmpm, [[-1, 16]], ALU.is_equal, -16 * q, 1)
            pool(lambda: nc.gpsimd.tensor_tensor(out=RmF, in0=RmF, in1=tmpm, op=ALU.add), [cB], [cB])
        mh = cst[0:64, 125:126]
        ml = cst[0:64, 126:127]
        t1 = cst[0:64, 160:161]
        pool(lambda: nc.gpsimd.memset(mh, 1.0), [], [cB])
        asel(mh, [[0, 1]], ALU.is_ge, 15, -1)
        pool(lambda: nc.gpsimd.memset(t1, 1.0), [], [cB])
        asel(t1, [[0, 1]], ALU.is_ge, -32, 1)
        asel(t1, [[0, 1]], ALU.is_ge, 47, -1)
        pool(lambda: nc.gpsimd.tensor_tensor(out=mh, in0=mh, in1=t1, op=ALU.add), [cB], [cB])
        pool(lambda: nc.gpsimd.memset(ml, 1.0), [], [cB])
        pool(lambda: nc.gpsimd.tensor_tensor(out=ml, in0=ml, in1=mh, op=ALU.subtract), [cB], [cB])
        pool(lambda: nc.gpsimd.memset(cst[:, 124:125], -0.5), [], [cB])
        pool(lambda: nc.gpsimd.memset(lhsT_d[0:64, :], 1.0), [], [cB])
        pool(lambda: nc.gpsimd.memset(lhsT_d[64:128, :], 0.0), [], [cB])
        pool(lambda: nc.gpsimd.memset(Rfl[64:128, :], 0.0), [], [B["Rfl"]])
        neghalf = cst[:, 124:125]
        epsc = cst[:, 127:128]
        pool(lambda: nc.gpsimd.memset(epsc, EPS), [], [cB])

        dve(lambda: nc.vector.tensor_copy(out=identB[:], in_=identF[:]), [cB], [cB])
        dve(lambda: nc.vector.tensor_scalar(out=hT[:, 768:896], in0=triF[:], scalar1=-1.0, scalar2=30000.0,
                                            op0=ALU.add, op1=ALU.mult), [cB], [cB])
        dve(lambda: nc.vector.tensor_copy(out=NEG4[:, :].rearrange("p (h l) -> p h l", l=128),
                                          in_=hT[:, 768:896].unsqueeze(1).to_broadcast([128, 4, 128])), [cB], [cB])
        dve(lambda: nc.vector.tensor_copy(out=RmB[:], in_=RmF), [cB], [cB])
        act(lambda: nc.scalar.activation(out=cst[:, 32:48], in_=cst[:, 32:48], func=AF.Exp), [cB], [cB], tab="exp")
        dve(lambda: nc.vector.tensor_scalar(out=cst[:, 32:48], in0=cst[:, 32:48], scalar1=-1.0, scalar2=None,
                                            op0=ALU.mult), [cB], [cB])
        a_bc = cst[:, 32:48]
        dtb_bc = cst[:, 48:64]
        for blk in range(12):
            src = Ft if blk < 6 else hT
            cc = (blk % 6) * 128
            pe(lambda src=src, cc=cc, blk=blk: nc.tensor.matmul(P_S[:, 64 + blk * 5:64 + (blk + 1) * 5], lhsT=src[0:5, cc:cc + 128],
                                                                 rhs=identF[0:5, 0:5], start=True, stop=True),
               [cB], [B["PS"]])
        dve(lambda: nc.vector.tensor_copy(out=cst[:, 64:124], in_=P_S[:, 64:124]), [B["PS"]], [cB])
        for j in range(8):
            dve(lambda j=j: nc.vector.tensor_scalar(out=diagD[:, j, :], in0=identF[:], scalar1=cst[:, 24 + j:25 + j],
                                                    scalar2=None, op0=ALU.mult), [cB], [cB])
        for h in range(8):
            pe(lambda h=h: nc.tensor.matmul(P_X[:, (h % 4) * 128:(h % 4 + 1) * 128] if h < 4 else
                                                       P_Y[:, (h % 4) * 128:(h % 4 + 1) * 128],
                                                       lhsT=XF[:, h * 128:(h + 1) * 128], rhs=identF[:],
                                                       start=True, stop=True),
               [cB], [B["PX"] if h < 4 else B["PY"]])
        dve(lambda: nc.vector.tensor_copy(out=WmT[:, 0:4, :], in_=P_X[:, 0:512].rearrange("p (h t) -> p h t", t=128)),
            [B["PX"]], [cB])
        dve(lambda: nc.vector.tensor_copy(out=WmT[:, 4:8, :], in_=P_Y[:, 0:512].rearrange("p (h t) -> p h t", t=128)),
            [B["PY"]], [cB])
        dve(lambda: nc.vector.memset(WmT[64:128, :, 0:64], 0.0), [cB], [cB])
        for b in [B["XF"], B["Ft"], B["hT"]]:
            b.r = set(cB.r)
            b.w = set(cB.w)

        for nm in ["Wi%d%s" % (pc, e) for pc in range(12) for e in "AD"] + ["WoA", "WoD", "FtL", "FtH", "hTL", "hTH"]:
            B[nm] = Buf(nm)
        f32v = lambda t: t[:, :].bitcast(F32)[:, 0:512]
        stgs = [(cvA[:, 0:512], [B["cvA"]]), (cvB[:, 0:512], [B["cvB"]]), (Ft[:, 0:512], [B["Ft"]]), (hT[:, 0:512], [B["hT"]]),
                (cvD[:, 0:512], [B["cvD0"], B["cvD1"], B["cvD2"], B["cvD3"]]),
                (f32v(H3a), [B["H3a"]]), (f32v(H3b), [B["H3b"]]), (f32v(aT), [B["aT"]]), (f32v(bT), [B["bT"]]),
                (f32v(H4), [B["H4"]]), (f32v(H1), [B["H1"]])]
        wi = [0]

        def wload(dst, src_ap, ncols, scale_ap, wname):
            st, stb = stgs[wi[0] % len(stgs)]
            eng = ("act", "dve")[wi[0] % 2]
            S.dma("stg%d" % (wi[0] % len(stgs)), st[:, 0:ncols], src_ap, [], stb)
            WB = B[wname + ("A" if eng == "act" else "D")]
            if eng == "act":
                if scale_ap is None:
                    act(lambda: nc.scalar.copy(out=dst, in_=st[:, 0:ncols]), stb + [B["cNin"]], [WB], n=ncols)
                else:
                    act(lambda: nc.scalar.activation(out=dst, in_=st[:, 0:ncols], func=AF.Identity, scale=scale_ap),
                        stb + [B["cNin"]], [WB], n=ncols)
            else:
                if scale_ap is None:
                    dve(lambda: nc.vector.tensor_copy(out=dst, in_=st[:, 0:ncols]), stb + [B["cNin"]], [WB], n=ncols // 2)
                else:
                    dve(lambda: nc.vector.tensor_scalar(out=dst, in0=st[:, 0:ncols], scalar1=scale_ap, scalar2=None,
                                                        op0=ALU.mult), stb + [B["cNin"]], [WB], n=ncols // 2)
            wi[0] += 1

        for pc in (8, 9, 10, 11, 6, 7, 4, 5, 0, 1, 2, 3):
            c0 = pc * 512
            w = min(512, DIN - c0)
            for k in range(8):
                wload(Win[:, k, c0:c0 + w], w_in[k * 128:(k + 1) * 128, c0:c0 + w], w, nin_pp[:, k:k + 1], "Wi%d" % pc)
        for k in range(16):
            for cg in range(2):
                wload(Wout[:, k, cg * 512:(cg + 1) * 512], w_out[k * 128:(k + 1) * 128, cg * 512:(cg + 1) * 512], 512,
                      None if k < 8 else snw_pp[:, k - 8:k - 7], "Wo")

        def rstd_from_ss(T, col, sB, rB):
            act(lambda: nc.scalar.activation(out=sm[:T, col + 1:col + 2], in_=sm[:T, col:col + 1], func=AF.Ln,
                                             scale=1.0 / 1024, bias=epsc[:T]), [sB, cB], [rB], tab="exp")
            act(lambda: nc.scalar.activation(out=sm[:T, col + 1:col + 2], in_=sm[:T, col + 1:col + 2], func=AF.Exp,
                                             scale=-0.5), [rB], [rB], tab="exp")

        v3 = lambda ap, inner: ap.rearrange("p (a b) -> p a b", b=inner)
        GROUPS = [("x", OX, 512), ("x", OX + 512, 512), ("x", OX + 1024, 512), ("d", ODT, 16),
                  ("z", OZ, 512), ("z", OZ + 512, 512), ("g", OG, 512), ("g", OG + 512, 512),
                  ("u", OU, 512), ("u", OU + 512, 512), ("v", OV, 512), ("v", OV + 512, 512)]
        KOFF = {"g": OG, "u": OU, "v": OV, "z": OZ, "x": OX, "d": ODT}
        W = [B["WinA"], B["WinD"]]
        rstate = {}

        def phase1(i, tt, next_load):
            T, last, conv_dst, v_dst = tt["T"], tt["last"], tt["conv_dst"], tt["v_dst"]
            first, init_conv = tt["first"], tt["init_conv"]
            p = i % 2
            F1t, F1B = F1b[p], B["F1b%d" % p]
            F3t, F3B = F3b[p], B["F3b%d" % p]
            H2t, H2B = H2[p], B["H2_%d" % p]
            xbt, xbB = xbc_bf, B["xbc"]
            actT, actB = actT2[p], B["actT%d" % p]
            rawv = v3(raw[:, 0:12 * (3 + T)], 3 + T)
            actv = v3(actT[:, 0:12 * T], T)
            dtc = 16 if p == 0 else 176
            dtB = B["dt%d" % p]
            chunks = []

            def c_init():
                if init_conv is None:
                    pool(lambda: nc.gpsimd.memset(raw[:], 0.0), [], [B["raw"]])
                    return
                S.dma("ldcs", cvA[0:3, 0:512], init_conv[:, 0:512], [], [B["cvA"]])
                S.dma("ldcs2", cvB[0:3, 0:512], init_conv[:, 512:1024], [], [B["cvB"]])
                act(lambda: nc.scalar.copy(out=xbt[0:3, 0:512], in_=cvA[0:3, 0:512]), [B["cvA"]], [xbB])
                act(lambda: nc.scalar.copy(out=xbt[0:3, 512:1024], in_=cvB[0:3, 0:512]), [B["cvB"]], [xbB])
                S.dma("ldcs3", cvD[0:3, 0:512], init_conv[:, 1024:1536], [], [B["cvD0"], B["cvD1"], B["cvD2"], B["cvD3"]])
                act(lambda: nc.scalar.copy(out=xbt[0:3, 1024:1536], in_=cvD[0:3, 0:512]), [B["cvD0"], B["cvD1"], B["cvD2"], B["cvD3"]], [xbB])
                for blk in range(12):
                    pe(lambda blk=blk: nc.tensor.transpose(out=P_T[:, blk * 4:blk * 4 + 3], in_=xbt[0:3, blk * 128:(blk + 1) * 128],
                                                           identity=identB[0:3, 0:3]), [xbB, cB], [B["PT"]], signal=(blk == 11), n=8)
                act(lambda: nc.scalar.copy(out=rawv[:, :, 0:3], in_=v3(P_T[:, 0:48], 4)[:, :, 0:3]), [B["PT"]], [B["raw"]])
            if first:
                chunks.append(c_init)

            def cA():
                act(lambda: nc.scalar.activation(out=H1x[:T, :], in_=XF[:T, :], func=AF.Square, accum_out=sm[:T, 0:1]),
                    [B["XF"]], [B["H1x"], B["ss1"]])
                rstd_from_ss(T, 0, B["ss1"], B["rs1"])
                dve(lambda: nc.vector.tensor_scalar(out=H1x[:T, :], in0=XF[:T, :], scalar1=sm[:T, 1:2], scalar2=None,
                                                    op0=ALU.mult), [B["XF"], B["rs1"]], [B["H1x"]], n=600)
                for k in range(8):
                    pe(lambda k=k: nc.tensor.transpose(out=P_T[:, k * T:(k + 1) * T], in_=H1x[:T, k * 128:(k + 1) * 128],
                                                       identity=identB[:T, :T]), [B["H1x"], cB], [B["PT"]], signal=(k == 7), n=T)
                act(lambda: nc.scalar.copy(out=xT[:, 0:8 * T], in_=P_T[:, 0:8 * T]), [B["PT"]], [B["xT"]])
                if next_load is not None:
                    next_load()
            chunks.append(cA)

            def mk_group(gi, kind, c0, w):
                def cG():
                    half = gi % 2
                    ipB = B["ip%d" % half]
                    pst = P_IP[:T, half * 512:half * 512 + w]
                    for k in range(8):
                        pe(lambda k=k: nc.tensor.matmul(pst, lhsT=xT[:, k * T:(k + 1) * T], rhs=Win[:, k, c0:c0 + w],
                                                        start=(k == 0), stop=(k == 7)),
                           [B["xT"], B["Wi%dA" % (c0 // 512)], B["Wi%dD" % (c0 // 512)]], [ipB], signal=(k == 7))
                    lc = c0 - KOFF[kind]
                    if kind == "g":
                        act(lambda: nc.scalar.activation(out=F1t[:T, lc:lc + 512], in_=pst, func=AF.Silu), [ipB], [F1B], n=512, tab="silu")
                    elif kind == "u":
                        dve(lambda: nc.vector.tensor_tensor(out=F1t[:T, lc:lc + 512], in0=pst, in1=F1t[:T, lc:lc + 512],
                                                            op=ALU.mult), [ipB, F1B], [F1B], n=512)
                    elif kind == "z":
                        act(lambda: nc.scalar.activation(out=F3t[:T, lc:lc + 512], in_=pst, func=AF.Silu), [ipB], [F3B], n=512, tab="silu")
                    elif kind == "x":
                        act(lambda: nc.scalar.copy(out=xbt[:T, lc:lc + 512], in_=pst), [ipB], [xbB], n=512)
                        if last:
                            h0 = T // 2
                            pc = lc // 512
                            stg, stB = ((H1x, B["H1x"]), (F1t, F1B), (H2t, H2B))[pc]
                            s32 = stg[:, :].bitcast(F32)
                            act(lambda: nc.scalar.copy(out=s32[h0:T, 0:512], in_=P_IP[h0:T, half * 512:half * 512 + 512]),
                                [ipB], [stB], n=512)
                            S.dma("st_cv%d" % pc, conv_dst[:, lc:lc + 512], s32[T - 3:T, 0:512], [stB], [])
                    elif kind == "d":
                        dve(lambda: nc.vector.tensor_tensor(out=sm[:T, dtc:dtc + 16], in0=pst, in1=dtb_bc[:T, :], op=ALU.add),
                            [ipB, cB], [dtB], n=16)
                        act(lambda: nc.scalar.activation(out=sm[:T, dtc:dtc + 16], in_=sm[:T, dtc:dtc + 16], func=AF.Exp),
                            [dtB], [dtB], n=16, tab="exp")
                        act(lambda: nc.scalar.activation(out=sm[:T, dtc:dtc + 16], in_=sm[:T, dtc:dtc + 16], func=AF.Ln,
                                                         bias=1.0), [dtB], [dtB], n=16, tab="exp")
                    elif kind == "v" and lc == 512:
                        pall = P_IP[:T, :]
                        both = [B["ip0"], B["ip1"]]
                        vB = B["vst"]
                        act(lambda: nc.scalar.activation(out=H1x[:T, :], in_=pall, func=AF.Identity, accum_out=sm[:T, 4:5]),
                            both, [B["H1x"], vB])
                        act(lambda: nc.scalar.activation(out=H1x[:T, :], in_=pall, func=AF.Square, accum_out=sm[:T, 5:6]),
                            both, [B["H1x"], vB])
                        dve(lambda: nc.vector.tensor_scalar(out=sm[:T, 4:6], in0=sm[:T, 4:6], scalar1=1.0 / 1024,
                                                            scalar2=None, op0=ALU.mult), [vB], [vB], n=8)
                        dve(lambda: nc.vector.tensor_tensor(out=sm[:T, 6:7], in0=sm[:T, 4:5], in1=sm[:T, 4:5], op=ALU.mult),
                            [vB], [vB], n=8)
                        dve(lambda: nc.vector.tensor_tensor(out=sm[:T, 6:7], in0=sm[:T, 5:6], in1=sm[:T, 6:7],
                                                            op=ALU.subtract), [vB], [vB], n=8)
                        act(lambda: nc.scalar.activation(out=sm[:T, 7:8], in_=sm[:T, 6:7], func=AF.Ln, bias=epsc[:T]),
                            [vB, cB], [vB], n=8, tab="exp")
                        act(lambda: nc.scalar.activation(out=sm[:T, 7:8], in_=sm[:T, 7:8], func=AF.Exp, scale=-0.5), [vB], [vB], n=8, tab="exp")
                        dve(lambda: nc.vector.scalar_tensor_tensor(out=sm[:T, 8:9], in0=sm[:T, 4:5], scalar=-1.0,
                                                                   in1=sm[:T, 7:8], op0=ALU.mult, op1=ALU.mult), [vB], [vB], n=8)
                        for hf, (cv, cvBf) in enumerate(((cvA, B["cvA"]), (cvB, B["cvB"]))):
                            cs = slice(hf * 512, hf * 512 + 512)
                            act(lambda cv=cv, cs=cs: nc.scalar.activation(out=cv[:T, 0:512], in_=P_IP[:T, cs], func=AF.Identity,
                                                                          scale=sm[:T, 7:8], bias=sm[:T, 8:9]),
                                [B["ip%d" % hf], vB], [cvBf], n=512)
                            pool(lambda cv=cv, cs=cs: nc.gpsimd.tensor_tensor(out=cv[:T, 0:512], in0=cv[:T, 0:512],
                                                                              in1=lnw_bc[:T, cs], op=ALU.mult),
                                 [cvBf, cB], [cvBf], n=512)
                            if v_dst is None:
                                pool(lambda cv=cv, cs=cs: nc.gpsimd.tensor_tensor(out=H2t[:T, cs], in0=cv[:T, 0:512],
                                                                                  in1=lnb_bc[:T, cs], op=ALU.add),
                                     [cvBf, cB], [H2B], n=512)
                            else:
                                pool(lambda cv=cv, cs=cs: nc.gpsimd.tensor_tensor(out=cv[:T, 0:512], in0=cv[:T, 0:512],
                                                                                  in1=lnb_bc[:T, cs], op=ALU.add),
                                     [cvBf, cB], [cvBf], n=512)
                                act(lambda cv=cv, cs=cs: nc.scalar.copy(out=H2t[:T, cs], in_=cv[:T, 0:512]), [cvBf], [H2B], n=512)
                                S.dma("st_v%d" % hf, v_dst[:, cs], cv[:T, 0:512], [cvBf], [])
                return cG
            def mk_convT(b0, b1):
                def f():
                    nb = b1 - b0
                    for blk in range(b0, b1):
                        pe(lambda blk=blk: nc.tensor.transpose(out=P_T[:, (blk - b0) * T:(blk - b0 + 1) * T],
                                                               in_=xbt[:T, blk * 128:(blk + 1) * 128], identity=identB[:T, :T]),
                           [xbB, cB], [B["PT"]], signal=(blk == b1 - 1), n=T)
                    act(lambda: nc.scalar.copy(out=rawv[:, b0:b1, 3:3 + T], in_=v3(P_T[:, 0:nb * T], T)), [B["PT"]], [B["raw"]])
                return f

            def mk_conv(c):
                def f():
                    if c == 0:
                        bs_ = slice(0, 4)
                        accv = v3(cvA[:, 0:4 * T], T)
                        tmpv = v3(cvB[:, 0:4 * T], T)
                        wk = lambda k: cwb[:, bs_, k:k + 1].to_broadcast([128, 4, T])
                        pool(lambda: nc.gpsimd.tensor_tensor(out=accv, in0=rawv[:, bs_, 0:T], in1=wk(0), op=ALU.mult),
                             [B["raw"], cB], [B["cvA"]], n=4 * T)
                        for k in (1, 2, 3):
                            pool(lambda k=k: nc.gpsimd.tensor_tensor(out=tmpv, in0=rawv[:, bs_, k:k + T], in1=wk(k), op=ALU.mult),
                                 [B["raw"], cB], [B["cvB"]], n=4 * T)
                            pool(lambda: nc.gpsimd.tensor_tensor(out=accv, in0=accv, in1=tmpv, op=ALU.add),
                                 [B["cvA"], B["cvB"]], [B["cvA"]], n=4 * T)
                        pool(lambda: nc.gpsimd.tensor_tensor(out=accv, in0=accv, in1=wk(4), op=ALU.add), [B["cvA"], cB], [B["cvA"]],
                             n=4 * T)
                        act(lambda: nc.scalar.activation(out=actv[:, bs_, :], in_=accv, func=AF.Silu), [B["cvA"]], [actB],
                            n=4 * T, tab="silu")
                    else:
                        for j in range(4):
                            blk = 4 * c + j
                            aj = cvD[:, j * 128:j * 128 + T]
                            aB = B["cvD%d" % j]
                            dve(lambda blk=blk, aj=aj: nc.vector.tensor_scalar(out=aj, in0=rawv[:, blk, 0:T],
                                                                               scalar1=cwb[:, blk, 0:1], scalar2=None,
                                                                               op0=ALU.mult), [B["raw"], cB], [aB], n=T // 2)
                            for k in (1, 2, 3):
                                dve(lambda blk=blk, aj=aj, k=k: nc.vector.scalar_tensor_tensor(
                                    out=aj, in0=rawv[:, blk, k:k + T], scalar=cwb[:, blk, k:k + 1], in1=aj,
                                    op0=ALU.mult, op1=ALU.add), [B["raw"], aB, cB], [aB], n=T)
                            act(lambda blk=blk, aj=aj: nc.scalar.activation(out=actv[:, blk, :], in_=aj, func=AF.Silu,
                                                                            bias=cwb[:, blk, 4:5]), [aB, cB], [actB],
                                n=T, tab="silu")
                    if c == 2:
                        pool(lambda: nc.gpsimd.tensor_copy(out=rawv[:, :, 0:3], in_=rawv[:, :, T:T + 3]), [B["raw"]], [B["raw"]],
                             n=36)
                return f

            for gi, (kind, c0, w) in enumerate(GROUPS):
                chunks.append(mk_group(gi, kind, c0, w))
                if gi == 2:
                    chunks += [mk_convT(0, 8), mk_convT(8, 12), mk_conv(0), mk_conv(1), mk_conv(2)]
            return chunks

        def phase2(i, tt):
            T, first, last = tt["T"], tt["first"], tt["last"]
            ssm_dst, init_ssm, init_conv = tt["ssm_dst"], tt["init_ssm"], tt["init_conv"]
            x_src, y_dst = tt["x_src"], tt["y_dst"]
            p = i % 2
            F1t, F1B = F1b[p], B["F1b%d" % p]
            F3t, F3B = F3b[p], B["F3b%d" % p]
            H2t, H2B = H2[p], B["H2_%d" % p]
            actT, actB = actT2[p], B["actT%d" % p]
            dtc = 16 if p == 0 else 176
            dtB = B["dt%d" % p]
            dt_ap = sm[:T, dtc:dtc + 16]
            rawv = v3(raw[:, 0:12 * (3 + T)], 3 + T)
            actv = v3(actT[:, 0:12 * T], T)
            acum = sm[:T, 64:80]
            A4 = sm[:T, 64:128]
            Ms = [(H3a, B["H3a"]), (H3b, B["H3b"])]
            chunks = []

            def d_init():
                if not first:
                    return
                if init_ssm is None:
                    dve(lambda: nc.vector.memset(hT[:], 0.0), [], [B["hT"]])
                    dve(lambda: nc.vector.memset(hTb[:], 0.0), [], [B["hTb"]])
                else:
                    h3a32 = H3a[:, :].bitcast(F32)
                    h3b32 = H3b[:, :].bitcast(F32)
                    st3 = init_ssm.rearrange("(j p) n -> p j n", p=128)
                    S.dma("ldst", h3a32.rearrange("p (j n) -> p j n", n=128), st3[:, 0:4, :], [], [B["H3a"]])
                    S.dma("ldst2", h3b32.rearrange("p (j n) -> p j n", n=128), st3[:, 4:8, :], [], [B["H3b"]])
                    for j in range(8):
                        srcj, sBj = (h3a32, B["H3a"]) if j < 4 else (h3b32, B["H3b"])
                        jj = j % 4
                        pe(lambda j=j, srcj=srcj, jj=jj: nc.tensor.matmul(P_X[:, j * 128:(j + 1) * 128],
                                                                          lhsT=srcj[:, jj * 128:(jj + 1) * 128],
                                                                          rhs=identF[:], start=True, stop=True),
                           [sBj, cB], [B["PX"]], signal=(j == 7))
                    dve(lambda: nc.vector.tensor_copy(out=hT[:], in_=P_X[:]), [B["PX"]], [B["hT"]])
                    act(lambda: nc.scalar.copy(out=hTb[:], in_=hT[:]), [B["hT"]], [B["hTb"]])
            chunks.append(d_init)

            def d_gmlp():
                for h in range(8):
                    pe(lambda h=h: nc.tensor.matmul(P_X[:T, h * 128:(h + 1) * 128], lhsT=WmT[:T, h, :T],
                                                    rhs=H2t[:T, h * 128:(h + 1) * 128], start=True, stop=True),
                       [H2B, cB], [B["PX"]], signal=(h == 7), n=140)
                for h in range(8):
                    dve(lambda h=h: nc.vector.scalar_tensor_tensor(out=H1[:T, h * 128:(h + 1) * 128],
                                                                   in0=P_X[:T, h * 128:(h + 1) * 128], scalar=bsT[:T, h:h + 1],
                                                                   in1=F1t[:T, h * 128:(h + 1) * 128], op0=ALU.add, op1=ALU.mult),
                        [B["PX"], F1B, cB], [B["H1"]], n=128)
                for k in range(8):
                    pe(lambda k=k: nc.tensor.transpose(out=P_T[:, k * T:(k + 1) * T], in_=H1[:T, k * 128:(k + 1) * 128],
                                                       identity=identB[:T, :T]), [B["H1"], cB], [B["PT"]], signal=(k == 7), n=T)
                act(lambda: nc.scalar.copy(out=aT[:, 0:8 * T], in_=P_T[:, 0:8 * T]), [B["PT"]], [B["aT"]])

            def d_acum():
                dve(lambda: nc.vector.tensor_tensor(out=sm[:T, 32:48], in0=dt_ap, in1=a_bc[:T, :], op=ALU.mult),
                    [dtB, cB], [B["A4"]])
                pe(lambda: nc.tensor.matmul(P_S[:T, 0:16], lhsT=triF[:T, :T], rhs=sm[:T, 32:48], start=True, stop=True),
                   [B["A4"], cB], [B["PS"]])
                dve(lambda: nc.vector.tensor_copy(out=v3(sm[:T, 64:96], 16), in_=P_S[:T, 0:16].unsqueeze(1).to_broadcast([T, 2, 16])),
                    [B["PS"]], [B["A4"]])
                dve(lambda: nc.vector.tensor_scalar(out=v3(sm[:T, 96:128], 16), in0=P_S[:T, 0:16].unsqueeze(1).to_broadcast([T, 2, 16]),
                                                    scalar1=-1.0, scalar2=None, op0=ALU.mult), [B["PS"]], [B["A4"]])
                pe(lambda: nc.tensor.matmul(P_S[0:64, 128:128 + T], lhsT=A4, rhs=identF[:T, :T], start=True, stop=True),
                   [B["A4"], cB], [B["PS"]])
                ATp = P_S[0:64, 128:128 + T]
                hi_b = smb[:, 0:T]
                lo_b = smb[:, 128:128 + T]
                Vb = smb[:, 256:256 + T]
                hi_f = sm[0:64, 256:256 + T]
                dve(lambda: nc.vector.tensor_copy(out=hi_b, in_=ATp), [B["PS"]], [B["V"]])
                dve(lambda: nc.vector.tensor_copy(out=hi_f, in_=hi_b), [B["V"]], [B["V"]])
                dve(lambda: nc.vector.tensor_tensor(out=hi_f, in0=ATp, in1=hi_f, op=ALU.subtract), [B["PS"], B["V"]], [B["V"]])
                dve(lambda: nc.vector.tensor_scalar(out=lo_b, in0=hi_f, scalar1=ml, scalar2=None, op0=ALU.mult),
                    [B["V"], cB], [B["V"]])
                dve(lambda: nc.vector.scalar_tensor_tensor(out=Vb, in0=hi_b, scalar=mh, in1=lo_b, op0=ALU.mult, op1=ALU.add),
                    [B["V"], cB], [B["V"]])
                Rv = v3(Rfl[:, 0:16 * T], T)
                dve(lambda: nc.vector.tensor_tensor(out=Rv[0:32], in0=Vb[0:32].unsqueeze(1).to_broadcast([32, 16, T]),
                                                    in1=RmB[0:32, :].unsqueeze(2).to_broadcast([32, 16, T]), op=ALU.mult),
                    [B["V"], cB], [B["Rfl"]])
                if rstate.get("T") != T:
                    rstate["T"] = T
                    dve(lambda: nc.vector.tensor_copy(out=Rv[32:64], in_=RmB[32:64, :].unsqueeze(2).to_broadcast([32, 16, T])),
                        [cB], [B["Rfl"]])
                dve(lambda: nc.vector.tensor_copy(out=lhsT_d[32:64, 0:T], in_=Vb[32:64]), [B["V"]], [B["lhsT_d"]])
                pe(lambda: nc.tensor.matmul(P_S[:, 16:32], lhsT=onesF[:T, :], rhs=sm[:T, 32:48], start=True, stop=True),
                   [B["A4"], cB], [B["PS"]])
                dve(lambda: nc.vector.tensor_tensor(out=sm[:T, 144:160], in0=P_S[:T, 16:32], in1=acum, op=ALU.subtract),
                    [B["PS"], B["A4"]], [B["dte"]])
                dve(lambda: nc.vector.tensor_copy(out=sm[:, 160:176], in_=P_S[:, 16:32]), [B["PS"]], [B["cd"]])
                act(lambda: nc.scalar.activation(out=sm[:T, 144:160], in_=sm[:T, 144:160], func=AF.Exp), [B["dte"]], [B["dte"]], tab="exp")
                act(lambda: nc.scalar.activation(out=sm[:, 160:176], in_=sm[:, 160:176], func=AF.Exp), [B["cd"]], [B["cd"]], tab="exp")
                act(lambda: nc.scalar.activation(out=sm[:T, 128:144], in_=acum, func=AF.Exp), [B["A4"]], [B["Ee"]], tab="exp")

            def d_cb():
                for g in range(2):
                    pe(lambda g=g: nc.tensor.matmul(P_S[:T, 256 + g * T:256 + (g + 1) * T], lhsT=actv[:, 8 + g, :],
                                                    rhs=actv[:, 10 + g, :], start=True, stop=True),
                       [actB], [B["PS"]], signal=(g == 1), n=T)
                dve(lambda: nc.vector.tensor_tensor(out=v3(CBm[:T, 0:2 * T], T), in0=v3(P_S[:T, 256:256 + 2 * T], T),
                                                    in1=triF[:T, :T].unsqueeze(1).to_broadcast([T, 2, T]), op=ALU.mult),
                    [B["PS"], cB], [B["CBm"]])
                for j in range(8):
                    pe(lambda j=j: nc.tensor.transpose(out=P_T[:T, j * 128:(j + 1) * 128], in_=actv[:, j, :], identity=identB[:]),
                       [actB, cB], [B["PT"]], signal=(j == 7), n=140)
                dve(lambda: nc.vector.tensor_tensor(out=v3(H4[:T, :], 64), in0=v3(P_T[:T, :], 64),
                                                    in1=dt_ap.unsqueeze(2).to_broadcast([T, 16, 64]), op=ALU.mult),
                    [B["PT"], dtB], [B["H4"]])
                for g in range(2):
                    pe(lambda g=g: nc.tensor.transpose(out=P_T[:T, g * 128:(g + 1) * 128], in_=actv[:, 8 + g, :],
                                                       identity=identB[:]), [actB, cB], [B["PT"]], signal=(g == 1), n=140)
                act(lambda: nc.scalar.copy(out=Btm[:T, :], in_=P_T[:T, 0:256]), [B["PT"]], [B["Btm"]], n=256)

            def mk_decay(g):
                def f():
                    Pd, PdB = (P_Y, B["PY"]) if g == 0 else (P_X, B["PX"])
                    ncol = 8 * T
                    negv = v3(NEG4[:T, 0:512], 128)[:, :, 0:T] if T != 128 else NEG4[:T, 0:512]
                    Mt, MB = Ms[g]
                    for hc in range(2):
                        o3 = v3(Pd[:T, hc * 4 * T:(hc + 1) * 4 * T], T) if T != 128 else Pd[:T, hc * 512:(hc + 1) * 512]
                        r0 = (g * 8 + hc * 4) * T
                        pe(lambda o3=o3, r0=r0: nc.tensor.matmul(o3, lhsT=lhsT_d[:, 0:T],
                                                                 rhs=(v3(Rfl[:, r0:r0 + 4 * T], T) if T != 128 else Rfl[:, r0:r0 + 512]),
                                                                 start=True, stop=False),
                           [B["lhsT_d"], B["Rfl"], cB], [PdB], signal=False)
                        pe(lambda o3=o3: nc.tensor.matmul(o3, lhsT=identB[:T, :T], rhs=negv, start=False, stop=True),
                           [cB], [PdB], signal=(hc == 1))
                    act(lambda: nc.scalar.activation(out=Mt[:T, 0:ncol], in_=Pd[:T, 0:ncol], func=AF.Exp), [PdB], [MB], tab="exp")
                    dve(lambda: nc.vector.tensor_tensor(out=v3(Mt[:T, 0:ncol], T), in0=v3(Mt[:T, 0:ncol], T),
                                                        in1=CBm[:T, g * T:(g + 1) * T].unsqueeze(1).to_broadcast([T, 8, T]),
                                                        op=ALU.mult), [MB, B["CBm"]], [MB], n=600)
                return f

            def mk_ydiag(b):
                def f():
                    Mt, MB = Ms[b]
                    for jj in range(4):
                        j = 4 * b + jj
                        pe(lambda j=j, jj=jj: nc.tensor.matmul(P_Y[:T, j * 128:(j + 1) * 128], lhsT=actv[:, j, :],
                                                               rhs=diagD[:, j, :], start=(jj == 0), stop=False,
                                                               skip_group_check=True),
                           [actB, cB], [B["PY"]], signal=False, n=200)
                    for hh in range(8):
                        h = 8 * b + hh
                        pe(lambda h=h, hh=hh: nc.tensor.matmul(P_Y[:T, h * 64:(h + 1) * 64], lhsT=Mt[:T, hh * T:(hh + 1) * T],
                                                               rhs=H4[:T, h * 64:(h + 1) * 64], start=False, stop=(hh == 7),
                                                               skip_group_check=True),
                           [MB, B["H4"]], [B["PY"]], signal=(hh == 7), n=200)
                return f

            def d_yoff():
                for g in range(2):
                    pe(lambda g=g: nc.tensor.matmul(P_X[:T, g * 512:(g + 1) * 512], lhsT=actv[:, 10 + g, :],
                                                    rhs=hTb[:, g * 512:(g + 1) * 512], start=True, stop=True),
                       [actB, B["hTb"]], [B["PX"]], signal=(g == 1))
                dve(lambda: nc.vector.tensor_tensor(out=v3(Ft[:T, :], 64), in0=v3(P_X[:T, :], 64),
                                                    in1=sm[:T, 128:144].unsqueeze(2).to_broadcast([T, 16, 64]), op=ALU.mult),
                    [B["PX"], B["Ee"]], [B["Ft"]])
                dve(lambda: nc.vector.tensor_tensor(out=Ft[:T, :], in0=P_Y[:T, :], in1=Ft[:T, :], op=ALU.add),
                    [B["PY"], B["Ft"]], [B["Ft"]])
                dve(lambda: nc.vector.tensor_tensor(out=Ft[:T, :], in0=Ft[:T, :], in1=F3t[:T, :], op=ALU.mult),
                    [B["Ft"], F3B], [B["Ft"]])
                act(lambda: nc.scalar.activation(out=H1[:T, :], in_=Ft[:T, :], func=AF.Square, accum_out=sm[:T, 10:11]),
                    [B["Ft"]], [B["H1"], B["ss2"]])
                rstd_from_ss(T, 10, B["ss2"], B["rs2"])
                dve(lambda: nc.vector.tensor_scalar(out=H1[:T, :], in0=Ft[:T, :], scalar1=sm[:T, 11:12], scalar2=None,
                                                    op0=ALU.mult), [B["Ft"], B["rs2"]], [B["H1"]], n=600)
                S.dma("ld_xr", Ft[:T, :], x_src, [], [B["Ft"]])
                for k in range(8):
                    pe(lambda k=k: nc.tensor.transpose(out=P_T[:, k * T:(k + 1) * T], in_=H1[:T, k * 128:(k + 1) * 128],
                                                       identity=identB[:T, :T]), [B["H1"], cB], [B["PT"]], signal=(k == 7), n=T)
                act(lambda: nc.scalar.copy(out=bT[:, 0:8 * T], in_=P_T[:, 0:8 * T]), [B["PT"]], [B["bT"]])

            def d_state():
                pool(lambda: nc.gpsimd.tensor_tensor(out=v3(H4[:T, :], 64), in0=v3(H4[:T, :], 64),
                                                     in1=sm[:T, 144:160].unsqueeze(2).to_broadcast([T, 16, 64]), op=ALU.mult),
                     [B["H4"], B["dte"]], [B["H4"]])
                for g in range(2):
                    pe(lambda g=g: nc.tensor.matmul(P_Y[:, g * 512:(g + 1) * 512], lhsT=Btm[:T, g * 128:(g + 1) * 128],
                                                    rhs=H4[:T, g * 512:(g + 1) * 512], start=True, stop=True),
                       [B["Btm"], B["H4"]], [B["PY"]], signal=(g == 1))
                pool(lambda: nc.gpsimd.tensor_tensor(out=v3(hT[:, :], 64), in0=v3(hT[:, :], 64),
                                                     in1=sm[:, 160:176].unsqueeze(2).to_broadcast([128, 16, 64]), op=ALU.mult),
                     [B["hT"], B["cd"]], [B["hT"]])
                dve(lambda: nc.vector.tensor_tensor(out=hT[:, :], in0=P_Y[:, :], in1=hT[:, :], op=ALU.add),
                    [B["PY"], B["hT"]], [B["hT"]])
                if not last:
                    act(lambda: nc.scalar.copy(out=hTb[:, :], in_=hT[:, :]), [B["hT"]], [B["hTb"]])

            def d_out():
                for part, (src, sB_) in enumerate(((aT, B["aT"]), (bT, B["bT"]))):
                    for cg in range(2):
                        for kk in range(8):
                            k = part * 8 + kk
                            pe(lambda cg=cg, k=k, src=src, kk=kk: nc.tensor.matmul(P_X[:T, cg * 512:(cg + 1) * 512],
                                                                                   lhsT=src[:, kk * T:(kk + 1) * T],
                                                                                   rhs=Wout[:, k, cg * 512:(cg + 1) * 512],
                                                                                   start=(k == 0), stop=(k == 15),
                                                                                   skip_group_check=True),
                               [sB_, B["WoA"], B["WoD"]], [B["PX"]], signal=(kk == 7 and cg == 1))
                dve(lambda: nc.vector.tensor_tensor(out=Ft[:T, :], in0=P_X[:T, :], in1=Ft[:T, :], op=ALU.add),
                    [B["PX"], B["Ft"]], [B["Ft"]])
                act(lambda: nc.scalar.activation(out=H1[:T, :], in_=Ft[:T, :], func=AF.Square, accum_out=sm[:T, 10:11]),
                    [B["Ft"]], [B["H1"], B["ss2"]])
                rstd_from_ss(T, 10, B["ss2"], B["rs2"])
                dve(lambda: nc.vector.scalar_tensor_tensor(out=Ft[:T, :], in0=Ft[:T, :], scalar=sm[:T, 11:12], in1=nf_bc[:T, :],
                                                           op0=ALU.mult, op1=ALU.mult), [B["Ft"], B["rs2"], cB], [B["Ft"]])
                S.dma("st_y", y_dst, Ft[:T, :], [B["Ft"]], [])
                if last:
                    for j in range(8):
                        pe(lambda j=j: nc.tensor.matmul(P_Y[:, j * 128:(j + 1) * 128], lhsT=hT[:, j * 128:(j + 1) * 128],
                                                        rhs=identF[:], start=True, stop=True),
                           [B["hT"], cB], [B["PY"]], signal=(j == 7), n=512)
                    st3o = ssm_dst.rearrange("(j p) n -> p j n", p=128)
                    for hf, (stg, stB) in enumerate(((aT, B["aT"]), (bT, B["bT"]))):
                        s32 = stg[:, :].bitcast(F32)
                        dve(lambda hf=hf, s32=s32: nc.vector.tensor_copy(out=s32[:, 0:512], in_=P_Y[:, hf * 512:(hf + 1) * 512]),
                            [B["PY"]], [stB], n=512)
                        S.dma("st_ssm%d" % hf, st3o[:, hf * 4:(hf + 1) * 4, :], s32[:, 0:512].rearrange("p (j n) -> p j n", n=128),
                              [stB], [])

            chunks += [d_acum, d_gmlp, d_cb, mk_decay(0), mk_decay(1), mk_ydiag(0), mk_ydiag(1), d_yoff, d_state, d_out]
            return chunks

        tiles = []
        for t in range(SEQ // 128):
            tiles.append(dict(T=128, x_src=xp[t * 128:(t + 1) * 128, :], y_dst=yp[t * 128:(t + 1) * 128, :],
                              first=(t == 0), last=(t == SEQ // 128 - 1), conv_dst=oconv_p, ssm_dst=ossm_p, v_dst=None,
                              init_ssm=None, init_conv=None))
        for b in range(NS):
            tiles.append(dict(T=TS, x_src=xs[b], y_dst=ys[b], first=True, last=True, conv_dst=oconv_s[b], ssm_dst=ossm_s[b],
                              v_dst=ov_s[b], init_ssm=sssm[b], init_conv=sconv[b]))

        def mk_load(idx):
            def f():
                if idx < len(tiles):
                    tt = tiles[idx]
                    S.dma("ld_x", XF[:tt["T"], :], tt["x_src"], [], [B["XF"]])
            return f

        mk_load(0)()
        for c in phase1(0, tiles[0], mk_load(1)):
            c()
        for i in range(len(tiles)):
            c2 = phase2(i, tiles[i])
            c1 = phase1(i + 1, tiles[i + 1], mk_load(i + 2)) if i + 1 < len(tiles) else []
            n1, n2 = len(c1), len(c2)
            a = b_ = 0
            while a < n2 or b_ < n1:
                if b_ >= n1 or (a < n2 and a * max(n1, 1) <= b_ * n2):
                    c2[a]()
                    a += 1
                else:
                    c1[b_]()
                    b_ += 1

        S.flush()
        for key in list(S.sem.keys()):
            if key.startswith("st_"):
                S._wait("sp", key, S.cnt[key])
    return nc


_NC_CACHE = {}


def kernel(x_prompt, x_sample, state_ssm, state_conv, norm_in_w, w_in, gmlp_ln_w, gmlp_ln_b, gmlp_ws, gmlp_bs,
           conv_w, conv_b, dt_bias, a_log, d_skip, ssm_norm_w, w_out, norm_f_w):
    f = lambda a: np.ascontiguousarray(np.asarray(a, dtype=np.float32))
    n = 8
    if "nc" not in _NC_CACHE:
        _NC_CACHE["nc"] = build_nc()
    nc = _NC_CACHE["nc"]
    shared = {
        "norm_in_w": f(norm_in_w[0]), "w_in": f(w_in[0]), "gmlp_ln_w": f(gmlp_ln_w[0]), "gmlp_ln_b": f(gmlp_ln_b[0]),
        "gmlp_ws": f(gmlp_ws[0]), "gmlp_bs": f(gmlp_bs[0]), "conv_w": f(conv_w[0]), "conv_b": f(conv_b[0]),
        "dt_bias": f(dt_bias[0]), "a_log": f(a_log[0]), "d_skip": f(d_skip[0]), "ssm_norm_w": f(ssm_norm_w[0]),
        "w_out": f(w_out[0]), "norm_f_w": f(norm_f_w),
    }
    in_maps = []
    for c in range(n):
        m = dict(shared)
        m["xp"] = f(x_prompt[c])
        m["xs"] = f(x_sample[NS * c:NS * (c + 1)])
        m["sssm"] = f(np.asarray(state_ssm)[0, NS * c:NS * (c + 1)].reshape(NS, 1024, 128))
        m["sconv"] = f(np.asarray(state_conv)[0, NS * c:NS * (c + 1)])
        in_maps.append(m)
    res = run_bass_kernel_spmd(nc, in_maps, core_ids=list(range(n)))
    R = res.results
    y_prompt = np.stack([R[c]["yp"] for c in range(n)], 0)
    y_sample = np.concatenate([R[c]["ys"] for c in range(n)], 0)
    ssm_p = np.stack([R[c]["ossm_p"].reshape(16, 64, 128) for c in range(n)], 0)[None]
    conv_p = np.stack([R[c]["oconv_p"] for c in range(n)], 0)[None]
    ssm_s = np.concatenate([R[c]["ossm_s"].reshape(NS, 16, 64, 128) for c in range(n)], 0)[None]
    conv_s = np.concatenate([R[c]["oconv_s"] for c in range(n)], 0)[None]
    v_s = np.concatenate([R[c]["ov_s"] for c in range(n)], 0)[None]
    return (y_prompt.astype(np.float32), y_sample.astype(np.float32), ssm_p.astype(np.float32), conv_p.astype(np.float32),
            ssm_s.astype(np.float32), conv_s.astype(np.float32), v_s.astype(np.float32))
```

```python
import contextlib
import numpy as np
import concourse.bass as bass
import concourse.mybir as mybir
from concourse.bass_utils import run_bass_kernel_spmd

F32 = mybir.dt.float32
BF16 = mybir.dt.bfloat16
AF = mybir.ActivationFunctionType
ALU = mybir.AluOpType

D = 1024
SEQ = 4096
NS = 4
TS = 64
DIN = 5648
CONV = 1536
EPS = 1e-5
OU, OV, OG, OZ, OX, ODT = 0, 1024, 2048, 3072, 4096, 5632


class Buf:
    def __init__(self, name, excl=False):
        self.name = name
        self.w = set()
        self.r = set()
        self.xr = set()
        self.excl = excl


class Unit:
    __slots__ = ("eng", "fns", "reads", "writes", "cost", "deps", "key", "idx", "lat", "tab", "line")

    def __init__(self, eng, key=None):
        self.eng = eng
        self.fns = []
        self.reads = []
        self.writes = []
        self.cost = 0.0
        self.deps = set()
        self.key = key
        self.lat = 0.0
        self.tab = None
        self.line = 0


class Sched:
    XLAT = 0.8
    COST = {"pe": (0.015, 1.0 / 2300), "act": (0.2, 1.0 / 1200), "dve": (0.1, 1.0 / 920), "pool": (0.1, 1.0 / 480)}

    def __init__(self, nc, es):
        self.nc = nc
        self.es = es
        self.engs = {"pe": nc.tensor, "act": nc.scalar, "dve": nc.vector, "pool": nc.gpsimd, "sp": nc.sync}
        self.sem = {}
        self.cnt = {}
        self.waited = {e: {} for e in self.engs}
        for e in ("pe", "act", "dve", "pool"):
            self.newsem(e)
        self.units = []
        self.pending = {}
        self.last_dma = {}
        self.sigval = {}

    def newsem(self, key):
        self.sem[key] = self.es.enter_context(self.nc.semaphore("s_" + key))
        self.cnt[key] = 0

    def _close(self, u):
        deps = set()
        for b in u.reads:
            deps |= b.w
            if b.excl:
                deps |= {x for x in b.xr if self.units[x].eng != u.eng}
        for b in u.writes:
            deps |= b.w
            deps |= b.r
            deps |= b.xr
        u.idx = len(self.units)
        u.deps = deps
        for b in u.reads:
            if b.excl:
                b.xr.add(u.idx)
            else:
                b.r.add(u.idx)
        for b in u.writes:
            b.w = {u.idx}
            b.r = set()
            b.xr = set()
        self.units.append(u)

    def op(self, e, fn, reads=(), writes=(), signal=True, n=None, tab=None):
        u = self.pending.get(e)
        if u is None:
            u = Unit(e)
            self.pending[e] = u
        u.fns.append(fn)
        if not u.line:
            import sys as _sys
            u.line = _sys._getframe(2).f_lineno
        if tab is not None:
            u.tab = tab
        u.reads += list(reads)
        u.writes += list(writes)
        c0, c1 = self.COST[e]
        u.cost += c0 + c1 * (n if n is not None else (512 if e == "pe" else 1024))
        if signal:
            del self.pending[e]
            self._close(u)

    def dma(self, key, out, in_, reads=(), writes=(), **kw):
        if key not in self.sem:
            self.newsem(key)
        u = Unit("sp", key)
        import sys as _sys
        u.line = _sys._getframe(1).f_lineno
        u.fns.append(lambda: self.nc.sync.dma_start(out=out, in_=in_, **kw))
        u.reads = list(reads)
        u.writes = list(writes)
        u.cost = 0.15
        u.lat = 4.0
        self._close(u)
        if key in self.last_dma:
            u.deps.add(self.last_dma[key])
        self.last_dma[key] = u.idx

    def _wait(self, e, key, val):
        if val <= 0 or self.waited[e].get(key, 0) >= val:
            return
        self.engs[e].wait_ge(self.sem[key], val)
        self.waited[e][key] = val

    def flush(self):
        assert not self.pending, list(self.pending)
        units = self.units
        n = len(units)
        XL, SL, TSW, EPS_T = self.XLAT, 0.2, 0.0, 1.27
        users = [[] for _ in range(n)]
        for u in units:
            for d in u.deps:
                users[d].append(u.idx)
        bl = [0.0] * n
        for i in range(n - 1, -1, -1):
            u = units[i]
            m = 0.0
            for v in users[i]:
                c = bl[v] + (SL if units[v].eng == u.eng else XL)
                if c > m:
                    m = c
            bl[i] = u.cost + u.lat + m
        ndep = [len(u.deps) for u in units]
        rdy = [0.0] * n
        fin = [0.0] * n
        free = {e: 0.0 for e in self.engs}
        rel = {e: set() for e in self.engs}
        for i in range(n):
            if ndep[i] == 0:
                rel[units[i].eng].add(i)
        cur_tab = None
        order = []
        nleft = n
        while nleft:
            best_e, best_t = None, None
            for e, ss in rel.items():
                if not ss:
                    continue
                te = max(free[e], min(rdy[i] for i in ss))
                if best_t is None or te < best_t:
                    best_e, best_t = e, te
            e = best_e
            cand = [i for i in rel[e] if rdy[i] <= best_t + EPS_T]
            if e == "act" and cur_tab is not None:
                same = [i for i in cand if units[i].tab in (None, cur_tab)]
                if same:
                    cand = same
            i = max(cand, key=lambda j: (bl[j], -j))
            u = units[i]
            rel[e].remove(i)
            st = max(free[e], rdy[i])
            c = u.cost
            if u.tab is not None:
                if cur_tab is not None and u.tab != cur_tab:
                    c += TSW
                cur_tab = u.tab
            free[e] = st + c
            fin[i] = st + c + u.lat
            order.append(i)
            nleft -= 1
            for v in users[i]:
                t = fin[i] + (SL if units[v].eng == e else XL)
                if t > rdy[v]:
                    rdy[v] = t
                ndep[v] -= 1
                if ndep[v] == 0:
                    rel[units[v].eng].add(v)
        assert len(order) == n
        self.est_total = max(fin) if fin else 0.0
        for i in order:
            u = units[i]
            e = u.eng
            for d in sorted(u.deps):
                if units[d].eng == e and e == "pe":
                    continue
                key, val = self.sigval[d]
                self._wait(e, key, val)
            if e == "sp":
                ins = u.fns[0]()
                self.cnt[u.key] += 16
                ins.then_inc(self.sem[u.key], 16)
                self.sigval[i] = (u.key, self.cnt[u.key])
            else:
                ins = None
                for f in u.fns:
                    ins = f()
                self.cnt[e] += 1
                ins.then_inc(self.sem[e], 1)
                self.sigval[i] = (e, self.cnt[e])
        self.units = []


def build_nc(SEQ=SEQ, NS=NS):
    nc = bass.Bass("TRN2", target_bir_lowering=False)

    def din(name, shape):
        return nc.dram_tensor(name, list(shape), F32, kind="ExternalInput").ap()

    def dout(name, shape):
        return nc.dram_tensor(name, list(shape), F32, kind="ExternalOutput").ap()

    xp = din("xp", (SEQ, D))
    xs = din("xs", (NS, TS, D))
    sssm = din("sssm", (NS, 1024, 128))
    sconv = din("sconv", (NS, 3, CONV))
    nin = din("norm_in_w", (D,))
    w_in = din("w_in", (D, DIN))
    lnw = din("gmlp_ln_w", (D,))
    lnb = din("gmlp_ln_b", (D,))
    gws = din("gmlp_ws", (8, 128, 128))
    gbs = din("gmlp_bs", (8, 128))
    cvw = din("conv_w", (4, CONV))
    cvb = din("conv_b", (CONV,))
    dtb = din("dt_bias", (16,))
    alog = din("a_log", (16,))
    dsk = din("d_skip", (16,))
    snw = din("ssm_norm_w", (D,))
    w_out = din("w_out", (2048, D))
    nfw = din("norm_f_w", (D,))

    yp = dout("yp", (SEQ, D))
    ys = dout("ys", (NS, TS, D))
    ossm_p = dout("ossm_p", (1024, 128))
    oconv_p = dout("oconv_p", (3, CONV))
    ossm_s = dout("ossm_s", (NS, 1024, 128))
    oconv_s = dout("oconv_s", (NS, 3, CONV))
    ov_s = dout("ov_s", (NS, TS, D))

    es = contextlib.ExitStack()
    with es:
        S = Sched(nc, es)

        def sb(name, shape, dt):
            return es.enter_context(nc.sbuf_tensor(name, list(shape), dt))

        def ps(name, shape, dt):
            return es.enter_context(nc.psum_tensor(name, list(shape), dt))

        Win = sb("Win", (128, 8, DIN), BF16)
        Wout = sb("Wout", (128, 16, D), BF16)
        identF = sb("identF", (128, 128), F32)
        identB = sb("identB", (128, 128), BF16)
        triF = sb("triF", (128, 128), F32)
        onesF = sb("onesF", (128, 128), F32)
        NEG4 = sb("NEG4", (128, 512), BF16)
        WmT = sb("WmT", (128, 8, 128), BF16)
        lnw_bc = sb("lnw_bc", (128, D), F32)
        lnb_bc = sb("lnb_bc", (128, D), F32)
        nf_bc = sb("nf_bc", (128, D), F32)
        diagD = sb("diagD", (128, 8, 128), BF16)
        cst = sb("cst", (128, 256), F32)
        cwb = cst[:, 64:124].rearrange("p (b k) -> p b k", k=5)
        RmB = sb("RmB", (64, 16), BF16)
        lhsT_d = sb("lhsT_d", (128, 128), BF16)
        XF = sb("XF", (128, D), F32)
        Ft = sb("Ft", (128, D), F32)
        hT = sb("hT", (128, D), F32)
        H1x = sb("H1x", (128, D), BF16)
        xT = sb("xT", (128, D), BF16)
        F1b = [sb("F1b%d" % i, (128, D), BF16) for i in range(2)]
        F3b = [sb("F3b%d" % i, (128, D), BF16) for i in range(2)]
        H2 = [sb("H2_%d" % i, (128, D), BF16) for i in range(2)]
        xbc_bf = sb("xbc_bf", (128, CONV), BF16)
        H1 = sb("H1", (128, D), BF16)
        H3a = sb("H3a", (128, D), BF16)
        H3b = sb("H3b", (128, D), BF16)
        H4 = sb("H4", (128, D), BF16)
        aT = sb("aT", (128, D), BF16)
        bT = sb("bT", (128, D), BF16)
        hTb = sb("hTb", (128, D), BF16)
        raw = sb("raw", (128, 12 * 131), BF16)
        cvA = sb("cvA", (128, 512), F32)
        cvB = sb("cvB", (128, 512), F32)
        cvD = sb("cvD", (128, 512), F32)
        actT2 = [sb("actT%d" % i, (128, 12 * 128), BF16) for i in range(2)]
        Rfl = sb("Rfl", (128, 2048), BF16)
        CBm = sb("CBm", (128, 256), BF16)
        Btm = sb("Btm", (128, 256), BF16)
        sm = sb("sm", (128, 384), F32)
        smb = sb("smb", (64, 384), BF16)

        P_IP = ps("P_IP", (128, 1024), F32)
        P_X = ps("P_X", (128, 1024), F32)
        P_Y = ps("P_Y", (128, 1024), F32)
        P_T = ps("P_T", (128, 1024), BF16)
        P_S = ps("P_S", (128, 512), F32)

        B = {n: Buf(n) for n in (
            "WinA", "WinD", "const", "XF", "Ft", "hT", "H1x", "xT", "F1b0", "F1b1", "F3b0", "F3b1", "H2_0", "H2_1",
            "xbc", "actT0", "actT1", "H1", "H3a", "H3b", "H4", "aT", "bT", "hTb", "raw", "cvA", "cvB", "cvD0", "cvD1", "cvD2", "cvD3", "Rfl", "CBm", "Btm",
            "ip0", "ip1", "PX", "PY", "PT", "PS",
            "ss1", "rs1", "ss2", "rs2", "vst", "dt0", "dt1", "A4", "Ee", "dte", "cd", "V", "lhsT_d")}
        for n in ("ip0", "ip1", "PX", "PY", "PT", "PS"):
            B[n].excl = True

        S.newsem("const")
        nconst = [0]

        def cdma(out, in_, **kw):
            nc.sync.dma_start(out=out, in_=in_, **kw).then_inc(S.sem["const"], 16)
            nconst[0] += 16

        cdma(Ft[0:8, 768:896], gbs[:, :])
        cdma(Ft[8:16, 768:896], nin.rearrange("(k p) -> k p", p=128))
        cdma(Ft[16:24, 768:896], snw.rearrange("(k p) -> k p", p=128))
        cdma(Ft[0:4, 0:768], cvw[:, 0:768])
        cdma(hT[0:4, 0:768], cvw[:, 768:CONV])
        cdma(Ft[4:5, 0:768], cvb[0:768].rearrange("(o n) -> o n", o=1))
        cdma(hT[4:5, 0:768], cvb[768:CONV].rearrange("(o n) -> o n", o=1))
        cdma(lnw_bc[:], lnw.partition_broadcast(128))
        cdma(lnb_bc[:], lnb.partition_broadcast(128))
        cdma(nf_bc[:], nfw.partition_broadcast(128))
        cdma(cst[:, 32:48], alog.partition_broadcast(128))
        cdma(cst[:, 48:64], dtb.partition_broadcast(128))
        dsk2 = dsk.rearrange("(j t) -> t j", t=2)
        cdma(cst[0:64, 24:32], dsk2[0].partition_broadcast(64), allow_slow_non_contiguous=True)
        cdma(cst[64:128, 24:32], dsk2[1].partition_broadcast(64), allow_slow_non_contiguous=True)
        for h in range(8):
            cdma(XF[:, h * 128:(h + 1) * 128], gws[h])
        for e in ("pe", "act", "dve", "pool"):
            S.engs[e].wait_ge(S.sem["const"], nconst[0])
            S.waited[e]["const"] = nconst[0]
        S.waited["sp"]["const"] = 0
        S.cnt["const"] = nconst[0]

        cB = B["const"]
        def pool(fn, reads=(), writes=(), n=None):
            return S.op("pool", fn, reads, writes, True, n)

        def dve(fn, reads=(), writes=(), n=None):
            return S.op("dve", fn, reads, writes, True, n)

        def act(fn, reads=(), writes=(), n=None, tab=None):
            return S.op("act", fn, reads, writes, True, n, tab)

        def pe(fn, reads=(), writes=(), signal=True, n=None):
            return S.op("pe", fn, reads, writes, signal, n)

        def asel(t, pattern, cmp, base, cm):
            pool(lambda: nc.gpsimd.affine_select(out=t, in_=t, pattern=pattern, compare_op=cmp, fill=0.0,
                                                 base=base, channel_multiplier=cm), [cB], [cB])

        for t in (identF, triF, onesF):
            pool(lambda t=t: nc.gpsimd.memset(t[:], 1.0), [], [cB])
        asel(identF[:], [[-1, 128]], ALU.is_equal, 0, 1)
        B["cNin"] = Buf("cNin")
        pe(lambda: nc.tensor.matmul(P_S[:, 0:24], lhsT=Ft[0:24, 768:896], rhs=identF[0:24, 0:24], start=True, stop=True),
           [cB], [B["PS"]])
        dve(lambda: nc.vector.tensor_copy(out=cst[:, 0:24], in_=P_S[:, 0:24]), [B["PS"]], [cB, B["cNin"]])
        bsT = cst[:, 0:8]
        nin_pp = cst[:, 8:16]
        snw_pp = cst[:, 16:24]
        asel(triF[:], [[1, 128]], ALU.is_ge, 0, -1)
        RmF = cst[0:64, 128:144]
        tmpm = cst[0:64, 144:160]
        pool(lambda: nc.gpsimd.memset(RmF, 0.0), [], [cB])
        for q in range(4):
            pool(lambda: nc.gpsimd.memset(tmpm, 1.0), [], [cB])
            asel(tmpm, [[-1, 16]], ALU.is_equal, -16 * q, 1)
            pool(lambda: nc.gpsimd.tensor_tensor(out=RmF, in0=RmF, in1=tmpm, op=ALU.add), [cB], [cB])
        mh = cst[0:64, 125:126]
        ml = cst[0:64, 126:127]
        t1 = cst[0:64, 160:161]
        pool(lambda: nc.gpsimd.memset(mh, 1.0), [], [cB])
        asel(mh, [[0, 1]], ALU.is_ge, 15, -1)
        pool(lambda: nc.gpsimd.memset(t1, 1.0), [], [cB])
        asel(t1, [[0, 1]], ALU.is_ge, -32, 1)
        asel(t1, [[0, 1]], ALU.is_ge, 47, -1)
        pool(lambda: nc.gpsimd.tensor_tensor(out=mh, in0=mh, in1=t1, op=ALU.add), [cB], [cB])
        pool(lambda: nc.gpsimd.memset(ml, 1.0), [], [cB])
        pool(lambda: nc.gpsimd.tensor_tensor(out=ml, in0=ml, in1=mh, op=ALU.subtract), [cB], [cB])
        pool(lambda: nc.gpsimd.memset(cst[:, 124:125], -0.5), [], [cB])
        pool(lambda: nc.gpsimd.memset(lhsT_d[0:64, :], 1.0), [], [cB])
        pool(lambda: nc.gpsimd.memset(lhsT_d[64:128, :], 0.0), [], [cB])
        pool(lambda: nc.gpsimd.memset(Rfl[64:128, :], 0.0), [], [B["Rfl"]])
        neghalf = cst[:, 124:125]
        epsc = cst[:, 127:128]
        pool(lambda: nc.gpsimd.memset(epsc, EPS), [], [cB])

        dve(lambda: nc.vector.tensor_copy(out=identB[:], in_=identF[:]), [cB], [cB])
        dve(lambda: nc.vector.tensor_scalar(out=hT[:, 768:896], in0=triF[:], scalar1=-1.0, scalar2=30000.0,
                                            op0=ALU.add, op1=ALU.mult), [cB], [cB])
        dve(lambda: nc.vector.tensor_copy(out=NEG4[:, :].rearrange("p (h l) -> p h l", l=128),
                                          in_=hT[:, 768:896].unsqueeze(1).to_broadcast([128, 4, 128])), [cB], [cB])
        dve(lambda: nc.vector.tensor_copy(out=RmB[:], in_=RmF), [cB], [cB])
        act(lambda: nc.scalar.activation(out=cst[:, 32:48], in_=cst[:, 32:48], func=AF.Exp), [cB], [cB], tab="exp")
        dve(lambda: nc.vector.tensor_scalar(out=cst[:, 32:48], in0=cst[:, 32:48], scalar1=-1.0, scalar2=None,
                                            op0=ALU.mult), [cB], [cB])
        a_bc = cst[:, 32:48]
        dtb_bc = cst[:, 48:64]
        for blk in range(12):
            src = Ft if blk < 6 else hT
            cc = (blk % 6) * 128
            pe(lambda src=src, cc=cc, blk=blk: nc.tensor.matmul(P_S[:, 64 + blk * 5:64 + (blk + 1) * 5], lhsT=src[0:5, cc:cc + 128],
                                                                 rhs=identF[0:5, 0:5], start=True, stop=True),
               [cB], [B["PS"]])
        dve(lambda: nc.vector.tensor_copy(out=cst[:, 64:124], in_=P_S[:, 64:124]), [B["PS"]], [cB])
        for j in range(8):
            dve(lambda j=j: nc.vector.tensor_scalar(out=diagD[:, j, :], in0=identF[:], scalar1=cst[:, 24 + j:25 + j],
                                                    scalar2=None, op0=ALU.mult), [cB], [cB])
        for h in range(8):
            pe(lambda h=h: nc.tensor.matmul(P_X[:, (h % 4) * 128:(h % 4 + 1) * 128] if h < 4 else
                                                       P_Y[:, (h % 4) * 128:(h % 4 + 1) * 128],
                                                       lhsT=XF[:, h * 128:(h + 1) * 128], rhs=identF[:],
                                                       start=True, stop=True),
               [cB], [B["PX"] if h < 4 else B["PY"]])
        dve(lambda: nc.vector.tensor_copy(out=WmT[:, 0:4, :], in_=P_X[:, 0:512].rearrange("p (h t) -> p h t", t=128)),
            [B["PX"]], [cB])
        dve(lambda: nc.vector.tensor_copy(out=WmT[:, 4:8, :], in_=P_Y[:, 0:512].rearrange("p (h t) -> p h t", t=128)),
            [B["PY"]], [cB])
        dve(lambda: nc.vector.memset(WmT[64:128, :, 0:64], 0.0), [cB], [cB])
        for b in [B["XF"], B["Ft"], B["hT"]]:
            b.r = set(cB.r)
            b.w = set(cB.w)

        for nm in ["Wi%d%s" % (pc, e) for pc in range(12) for e in "AD"] + ["WoA", "WoD", "FtL", "FtH", "hTL", "hTH"]:
            B[nm] = Buf(nm)
        f32v = lambda t: t[:, :].bitcast(F32)[:, 0:512]
        stgs = [(cvA[:, 0:512], [B["cvA"]]), (cvB[:, 0:512], [B["cvB"]]), (Ft[:, 0:512], [B["Ft"]]), (hT[:, 0:512], [B["hT"]]),
                (cvD[:, 0:512], [B["cvD0"], B["cvD1"], B["cvD2"], B["cvD3"]]),
                (f32v(H3a), [B["H3a"]]), (f32v(H3b), [B["H3b"]]), (f32v(aT), [B["aT"]]), (f32v(bT), [B["bT"]]),
                (f32v(H4), [B["H4"]]), (f32v(H1), [B["H1"]])]
        wi = [0]

        def wload(dst, src_ap, ncols, scale_ap, wname):
            st, stb = stgs[wi[0] % len(stgs)]
            eng = ("act", "dve")[wi[0] % 2]
            S.dma("stg%d" % (wi[0] % len(stgs)), st[:, 0:ncols], src_ap, [], stb)
            WB = B[wname + ("A" if eng == "act" else "D")]
            if eng == "act":
                if scale_ap is None:
                    act(lambda: nc.scalar.copy(out=dst, in_=st[:, 0:ncols]), stb + [B["cNin"]], [WB], n=ncols)
                else:
                    act(lambda: nc.scalar.activation(out=dst, in_=st[:, 0:ncols], func=AF.Identity, scale=scale_ap),
                        stb + [B["cNin"]], [WB], n=ncols)
            else:
                if scale_ap is None:
                    dve(lambda: nc.vector.tensor_copy(out=dst, in_=st[:, 0:ncols]), stb + [B["cNin"]], [WB], n=ncols // 2)
                else:
                    dve(lambda: nc.vector.tensor_scalar(out=dst, in0=st[:, 0:ncols], scalar1=scale_ap, scalar2=None,
                                                        op0=ALU.mult), stb + [B["cNin"]], [WB], n=ncols // 2)
            wi[0] += 1

        for pc in (8, 9, 10, 11, 6, 7, 4, 5, 0, 1, 2, 3):
            c0 = pc * 512
            w = min(512, DIN - c0)
            for k in range(8):
                wload(Win[:, k, c0:c0 + w], w_in[k * 128:(k + 1) * 128, c0:c0 + w], w, nin_pp[:, k:k + 1], "Wi%d" % pc)
        for k in range(16):
            for cg in range(2):
                wload(Wout[:, k, cg * 512:(cg + 1) * 512], w_out[k * 128:(k + 1) * 128, cg * 512:(cg + 1) * 512], 512,
                      None if k < 8 else snw_pp[:, k - 8:k - 7], "Wo")

        def rstd_from_ss(T, col, sB, rB):
            act(lambda: nc.scalar.activation(out=sm[:T, col + 1:col + 2], in_=sm[:T, col:col + 1], func=AF.Ln,
                                             scale=1.0 / 1024, bias=epsc[:T]), [sB, cB], [rB], tab="exp")
            act(lambda: nc.scalar.activation(out=sm[:T, col + 1:col + 2], in_=sm[:T, col + 1:col + 2], func=AF.Exp,
                                             scale=-0.5), [rB], [rB], tab="exp")

        v3 = lambda ap, inner: ap.rearrange("p (a b) -> p a b", b=inner)
        GROUPS = [("x", OX, 512), ("x", OX + 512, 512), ("x", OX + 1024, 512), ("d", ODT, 16),
                  ("z", OZ, 512), ("z", OZ + 512, 512), ("g", OG, 512), ("g", OG + 512, 512),
                  ("u", OU, 512), ("u", OU + 512, 512), ("v", OV, 512), ("v", OV + 512, 512)]
        KOFF = {"g": OG, "u": OU, "v": OV, "z": OZ, "x": OX, "d": ODT}
        W = [B["WinA"], B["WinD"]]
        rstate = {}

        def phase1(i, tt, next_load):
            T, last, conv_dst, v_dst = tt["T"], tt["last"], tt["conv_dst"], tt["v_dst"]
            first, init_conv = tt["first"], tt["init_conv"]
            p = i % 2
            F1t, F1B = F1b[p], B["F1b%d" % p]
            F3t, F3B = F3b[p], B["F3b%d" % p]
            H2t, H2B = H2[p], B["H2_%d" % p]
            xbt, xbB = xbc_bf, B["xbc"]
            actT, actB = actT2[p], B["actT%d" % p]
            rawv = v3(raw[:, 0:12 * (3 + T)], 3 + T)
            actv = v3(actT[:, 0:12 * T], T)
            dtc = 16 if p == 0 else 176
            dtB = B["dt%d" % p]
            chunks = []

            def c_init():
                if init_conv is None:
                    pool(lambda: nc.gpsimd.memset(raw[:], 0.0), [], [B["raw"]])
                    return
                S.dma("ldcs", cvA[0:3, 0:512], init_conv[:, 0:512], [], [B["cvA"]])
                S.dma("ldcs2", cvB[0:3, 0:512], init_conv[:, 512:1024], [], [B["cvB"]])
                act(lambda: nc.scalar.copy(out=xbt[0:3, 0:512], in_=cvA[0:3, 0:512]), [B["cvA"]], [xbB])
                act(lambda: nc.scalar.copy(out=xbt[0:3, 512:1024], in_=cvB[0:3, 0:512]), [B["cvB"]], [xbB])
                S.dma("ldcs3", cvD[0:3, 0:512], init_conv[:, 1024:1536], [], [B["cvD0"], B["cvD1"], B["cvD2"], B["cvD3"]])
                act(lambda: nc.scalar.copy(out=xbt[0:3, 1024:1536], in_=cvD[0:3, 0:512]), [B["cvD0"], B["cvD1"], B["cvD2"], B["cvD3"]], [xbB])
                for blk in range(12):
                    pe(lambda blk=blk: nc.tensor.transpose(out=P_T[:, blk * 4:blk * 4 + 3], in_=xbt[0:3, blk * 128:(blk + 1) * 128],
                                                           identity=identB[0:3, 0:3]), [xbB, cB], [B["PT"]], signal=(blk == 11), n=8)
                act(lambda: nc.scalar.copy(out=rawv[:, :, 0:3], in_=v3(P_T[:, 0:48], 4)[:, :, 0:3]), [B["PT"]], [B["raw"]])
            if first:
                chunks.append(c_init)

            def cA():
                act(lambda: nc.scalar.activation(out=H1x[:T, :], in_=XF[:T, :], func=AF.Square, accum_out=sm[:T, 0:1]),
                    [B["XF"]], [B["H1x"], B["ss1"]])
                rstd_from_ss(T, 0, B["ss1"], B["rs1"])
                dve(lambda: nc.vector.tensor_scalar(out=H1x[:T, :], in0=XF[:T, :], scalar1=sm[:T, 1:2], scalar2=None,
                                                    op0=ALU.mult), [B["XF"], B["rs1"]], [B["H1x"]], n=600)
                for k in range(8):
                    pe(lambda k=k: nc.tensor.transpose(out=P_T[:, k * T:(k + 1) * T], in_=H1x[:T, k * 128:(k + 1) * 128],
                                                       identity=identB[:T, :T]), [B["H1x"], cB], [B["PT"]], signal=(k == 7), n=T)
                act(lambda: nc.scalar.copy(out=xT[:, 0:8 * T], in_=P_T[:, 0:8 * T]), [B["PT"]], [B["xT"]])
                if next_load is not None:
                    next_load()
            chunks.append(cA)

            def mk_group(gi, kind, c0, w):
                def cG():
                    half = gi % 2
                    ipB = B["ip%d" % half]
                    pst = P_IP[:T, half * 512:half * 512 + w]
                    for k in range(8):
                        pe(lambda k=k: nc.tensor.matmul(pst, lhsT=xT[:, k * T:(k + 1) * T], rhs=Win[:, k, c0:c0 + w],
                                                        start=(k == 0), stop=(k == 7)),
                           [B["xT"], B["Wi%dA" % (c0 // 512)], B["Wi%dD" % (c0 // 512)]], [ipB], signal=(k == 7))
                    lc = c0 - KOFF[kind]
                    if kind == "g":
                        act(lambda: nc.scalar.activation(out=F1t[:T, lc:lc + 512], in_=pst, func=AF.Silu), [ipB], [F1B], n=512, tab="silu")
                    elif kind == "u":
                        dve(lambda: nc.vector.tensor_tensor(out=F1t[:T, lc:lc + 512], in0=pst, in1=F1t[:T, lc:lc + 512],
                                                            op=ALU.mult), [ipB, F1B], [F1B], n=512)
                    elif kind == "z":
                        act(lambda: nc.scalar.activation(out=F3t[:T, lc:lc + 512], in_=pst, func=AF.Silu), [ipB], [F3B], n=512, tab="silu")
                    elif kind == "x":
                        act(lambda: nc.scalar.copy(out=xbt[:T, lc:lc + 512], in_=pst), [ipB], [xbB], n=512)
                        if last:
                            h0 = T // 2
                            pc = lc // 512
                            stg, stB = ((H1x, B["H1x"]), (F1t, F1B), (H2t, H2B))[pc]
                            s32 = stg[:, :].bitcast(F32)
                            act(lambda: nc.scalar.copy(out=s32[h0:T, 0:512], in_=P_IP[h0:T, half * 512:half * 512 + 512]),
                                [ipB], [stB], n=512)
                            S.dma("st_cv%d" % pc, conv_dst[:, lc:lc + 512], s32[T - 3:T, 0:512], [stB], [])
                    elif kind == "d":
                        dve(lambda: nc.vector.tensor_tensor(out=sm[:T, dtc:dtc + 16], in0=pst, in1=dtb_bc[:T, :], op=ALU.add),
                            [ipB, cB], [dtB], n=16)
                        act(lambda: nc.scalar.activation(out=sm[:T, dtc:dtc + 16], in_=sm[:T, dtc:dtc + 16], func=AF.Exp),
                            [dtB], [dtB], n=16, tab="exp")
                        act(lambda: nc.scalar.activation(out=sm[:T, dtc:dtc + 16], in_=sm[:T, dtc:dtc + 16], func=AF.Ln,
                                                         bias=1.0), [dtB], [dtB], n=16, tab="exp")
                    elif kind == "v" and lc == 512:
                        pall = P_IP[:T, :]
                        both = [B["ip0"], B["ip1"]]
                        vB = B["vst"]
                        act(lambda: nc.scalar.activation(out=H1x[:T, :], in_=pall, func=AF.Identity, accum_out=sm[:T, 4:5]),
                            both, [B["H1x"], vB])
                        act(lambda: nc.scalar.activation(out=H1x[:T, :], in_=pall, func=AF.Square, accum_out=sm[:T, 5:6]),
                            both, [B["H1x"], vB])
                        dve(lambda: nc.vector.tensor_scalar(out=sm[:T, 4:6], in0=sm[:T, 4:6], scalar1=1.0 / 1024,
                                                            scalar2=None, op0=ALU.mult), [vB], [vB], n=8)
                        dve(lambda: nc.vector.tensor_tensor(out=sm[:T, 6:7], in0=sm[:T, 4:5], in1=sm[:T, 4:5], op=ALU.mult),
                            [vB], [vB], n=8)
                        dve(lambda: nc.vector.tensor_tensor(out=sm[:T, 6:7], in0=sm[:T, 5:6], in1=sm[:T, 6:7],
                                                            op=ALU.subtract), [vB], [vB], n=8)
                        act(lambda: nc.scalar.activation(out=sm[:T, 7:8], in_=sm[:T, 6:7], func=AF.Ln, bias=epsc[:T]),
                            [vB, cB], [vB], n=8, tab="exp")
                        act(lambda: nc.scalar.activation(out=sm[:T, 7:8], in_=sm[:T, 7:8], func=AF.Exp, scale=-0.5), [vB], [vB], n=8, tab="exp")
                        dve(lambda: nc.vector.scalar_tensor_tensor(out=sm[:T, 8:9], in0=sm[:T, 4:5], scalar=-1.0,
                                                                   in1=sm[:T, 7:8], op0=ALU.mult, op1=ALU.mult), [vB], [vB], n=8)
                        for hf, (cv, cvBf) in enumerate(((cvA, B["cvA"]), (cvB, B["cvB"]))):
                            cs = slice(hf * 512, hf * 512 + 512)
                            act(lambda cv=cv, cs=cs: nc.scalar.activation(out=cv[:T, 0:512], in_=P_IP[:T, cs], func=AF.Identity,
                                                                          scale=sm[:T, 7:8], bias=sm[:T, 8:9]),
                                [B["ip%d" % hf], vB], [cvBf], n=512)
                            pool(lambda cv=cv, cs=cs: nc.gpsimd.tensor_tensor(out=cv[:T, 0:512], in0=cv[:T, 0:512],
                                                                              in1=lnw_bc[:T, cs], op=ALU.mult),
                                 [cvBf, cB], [cvBf], n=512)
                            if v_dst is None:
                                pool(lambda cv=cv, cs=cs: nc.gpsimd.tensor_tensor(out=H2t[:T, cs], in0=cv[:T, 0:512],
                                                                                  in1=lnb_bc[:T, cs], op=ALU.add),
                                     [cvBf, cB], [H2B], n=512)
                            else:
                                pool(lambda cv=cv, cs=cs: nc.gpsimd.tensor_tensor(out=cv[:T, 0:512], in0=cv[:T, 0:512],
                                                                                  in1=lnb_bc[:T, cs], op=ALU.add),
                                     [cvBf, cB], [cvBf], n=512)
                                act(lambda cv=cv, cs=cs: nc.scalar.copy(out=H2t[:T, cs], in_=cv[:T, 0:512]), [cvBf], [H2B], n=512)
                                S.dma("st_v%d" % hf, v_dst[:, cs], cv[:T, 0:512], [cvBf], [])
                return cG
            def mk_convT(b0, b1):
                def f():
                    nb = b1 - b0
                    for blk in range(b0, b1):
                        pe(lambda blk=blk: nc.tensor.transpose(out=P_T[:, (blk - b0) * T:(blk - b0 + 1) * T],
                                                               in_=xbt[:T, blk * 128:(blk + 1) * 128], identity=identB[:T, :T]),
                           [xbB, cB], [B["PT"]], signal=(blk == b1 - 1), n=T)
                    act(lambda: nc.scalar.copy(out=rawv[:, b0:b1, 3:3 + T], in_=v3(P_T[:, 0:nb * T], T)), [B["PT"]], [B["raw"]])
                return f

            def mk_conv(c):
                def f():
                    if c == 0:
                        bs_ = slice(0, 4)
                        accv = v3(cvA[:, 0:4 * T], T)
                        tmpv = v3(cvB[:, 0:4 * T], T)
                        wk = lambda k: cwb[:, bs_, k:k + 1].to_broadcast([128, 4, T])
                        pool(lambda: nc.gpsimd.tensor_tensor(out=accv, in0=rawv[:, bs_, 0:T], in1=wk(0), op=ALU.mult),
                             [B["raw"], cB], [B["cvA"]], n=4 * T)
                        for k in (1, 2, 3):
                            pool(lambda k=k: nc.gpsimd.tensor_tensor(out=tmpv, in0=rawv[:, bs_, k:k + T], in1=wk(k), op=ALU.mult),
                                 [B["raw"], cB], [B["cvB"]], n=4 * T)
                            pool(lambda: nc.gpsimd.tensor_tensor(out=accv, in0=accv, in1=tmpv, op=ALU.add),
                                 [B["cvA"], B["cvB"]], [B["cvA"]], n=4 * T)
                        pool(lambda: nc.gpsimd.tensor_tensor(out=accv, in0=accv, in1=wk(4), op=ALU.add), [B["cvA"], cB], [B["cvA"]],
                             n=4 * T)
                        act(lambda: nc.scalar.activation(out=actv[:, bs_, :], in_=accv, func=AF.Silu), [B["cvA"]], [actB],
                            n=4 * T, tab="silu")
                    else:
                        for j in range(4):
                            blk = 4 * c + j
                            aj = cvD[:, j * 128:j * 128 + T]
                            aB = B["cvD%d" % j]
                            dve(lambda blk=blk, aj=aj: nc.vector.tensor_scalar(out=aj, in0=rawv[:, blk, 0:T],
                                                                               scalar1=cwb[:, blk, 0:1], scalar2=None,
                                                                               op0=ALU.mult), [B["raw"], cB], [aB], n=T // 2)
                            for k in (1, 2, 3):
                                dve(lambda blk=blk, aj=aj, k=k: nc.vector.scalar_tensor_tensor(
                                    out=aj, in0=rawv[:, blk, k:k + T], scalar=cwb[:, blk, k:k + 1], in1=aj,
                                    op0=ALU.mult, op1=ALU.add), [B["raw"], aB, cB], [aB], n=T)
                            act(lambda blk=blk, aj=aj: nc.scalar.activation(out=actv[:, blk, :], in_=aj, func=AF.Silu,
                                                                            bias=cwb[:, blk, 4:5]), [aB, cB], [actB],
                                n=T, tab="silu")
                    if c == 2:
                        pool(lambda: nc.gpsimd.tensor_copy(out=rawv[:, :, 0:3], in_=rawv[:, :, T:T + 3]), [B["raw"]], [B["raw"]],
                             n=36)
                return f

            for gi, (kind, c0, w) in enumerate(GROUPS):
                chunks.append(mk_group(gi, kind, c0, w))
                if gi == 2:
                    chunks += [mk_convT(0, 8), mk_convT(8, 12), mk_conv(0), mk_conv(1), mk_conv(2)]
            return chunks

        def phase2(i, tt):
            T, first, last = tt["T"], tt["first"], tt["last"]
            ssm_dst, init_ssm, init_conv = tt["ssm_dst"], tt["init_ssm"], tt["init_conv"]
            x_src, y_dst = tt["x_src"], tt["y_dst"]
            p = i % 2
            F1t, F1B = F1b[p], B["F1b%d" % p]
            F3t, F3B = F3b[p], B["F3b%d" % p]
            H2t, H2B = H2[p], B["H2_%d" % p]
            actT, actB = actT2[p], B["actT%d" % p]
            dtc = 16 if p == 0 else 176
            dtB = B["dt%d" % p]
            dt_ap = sm[:T, dtc:dtc + 16]
            rawv = v3(raw[:, 0:12 * (3 + T)], 3 + T)
            actv = v3(actT[:, 0:12 * T], T)
            acum = sm[:T, 64:80]
            A4 = sm[:T, 64:128]
            Ms = [(H3a, B["H3a"]), (H3b, B["H3b"])]
            chunks = []

            def d_init():
                if not first:
                    return
                if init_ssm is None:
                    dve(lambda: nc.vector.memset(hT[:], 0.0), [], [B["hT"]])
                    dve(lambda: nc.vector.memset(hTb[:], 0.0), [], [B["hTb"]])
                else:
                    h3a32 = H3a[:, :].bitcast(F32)
                    h3b32 = H3b[:, :].bitcast(F32)
                    st3 = init_ssm.rearrange("(j p) n -> p j n", p=128)
                    S.dma("ldst", h3a32.rearrange("p (j n) -> p j n", n=128), st3[:, 0:4, :], [], [B["H3a"]])
                    S.dma("ldst2", h3b32.rearrange("p (j n) -> p j n", n=128), st3[:, 4:8, :], [], [B["H3b"]])
                    for j in range(8):
                        srcj, sBj = (h3a32, B["H3a"]) if j < 4 else (h3b32, B["H3b"])
                        jj = j % 4
                        pe(lambda j=j, srcj=srcj, jj=jj: nc.tensor.matmul(P_X[:, j * 128:(j + 1) * 128],
                                                                          lhsT=srcj[:, jj * 128:(jj + 1) * 128],
                                                                          rhs=identF[:], start=True, stop=True),
                           [sBj, cB], [B["PX"]], signal=(j == 7))
                    dve(lambda: nc.vector.tensor_copy(out=hT[:], in_=P_X[:]), [B["PX"]], [B["hT"]])
                    act(lambda: nc.scalar.copy(out=hTb[:], in_=hT[:]), [B["hT"]], [B["hTb"]])
            chunks.append(d_init)

            def d_gmlp():
                for h in range(8):
                    pe(lambda h=h: nc.tensor.matmul(P_X[:T, h * 128:(h + 1) * 128], lhsT=WmT[:T, h, :T],
                                                    rhs=H2t[:T, h * 128:(h + 1) * 128], start=True, stop=True),
                       [H2B, cB], [B["PX"]], signal=(h == 7), n=140)
                for h in range(8):
                    dve(lambda h=h: nc.vector.scalar_tensor_tensor(out=H1[:T, h * 128:(h + 1) * 128],
                                                                   in0=P_X[:T, h * 128:(h + 1) * 128], scalar=bsT[:T, h:h + 1],
                                                                   in1=F1t[:T, h * 128:(h + 1) * 128], op0=ALU.add, op1=ALU.mult),
                        [B["PX"], F1B, cB], [B["H1"]], n=128)
                for k in range(8):
                    pe(lambda k=k: nc.tensor.transpose(out=P_T[:, k * T:(k + 1) * T], in_=H1[:T, k * 128:(k + 1) * 128],
                                                       identity=identB[:T, :T]), [B["H1"], cB], [B["PT"]], signal=(k == 7), n=T)
                act(lambda: nc.scalar.copy(out=aT[:, 0:8 * T], in_=P_T[:, 0:8 * T]), [B["PT"]], [B["aT"]])

            def d_acum():
                dve(lambda: nc.vector.tensor_tensor(out=sm[:T, 32:48], in0=dt_ap, in1=a_bc[:T, :], op=ALU.mult),
                    [dtB, cB], [B["A4"]])
                pe(lambda: nc.tensor.matmul(P_S[:T, 0:16], lhsT=triF[:T, :T], rhs=sm[:T, 32:48], start=True, stop=True),
                   [B["A4"], cB], [B["PS"]])
                dve(lambda: nc.vector.tensor_copy(out=v3(sm[:T, 64:96], 16), in_=P_S[:T, 0:16].unsqueeze(1).to_broadcast([T, 2, 16])),
                    [B["PS"]], [B["A4"]])
                dve(lambda: nc.vector.tensor_scalar(out=v3(sm[:T, 96:128], 16), in0=P_S[:T, 0:16].unsqueeze(1).to_broadcast([T, 2, 16]),
                                                    scalar1=-1.0, scalar2=None, op0=ALU.mult), [B["PS"]], [B["A4"]])
                pe(lambda: nc.tensor.matmul(P_S[0:64, 128:128 + T], lhsT=A4, rhs=identF[:T, :T], start=True, stop=True),
                   [B["A4"], cB], [B["PS"]])
                ATp = P_S[0:64, 128:128 + T]
                hi_b = smb[:, 0:T]
                lo_b = smb[:, 128:128 + T]
                Vb = smb[:, 256:256 + T]
                hi_f = sm[0:64, 256:256 + T]
                dve(lambda: nc.vector.tensor_copy(out=hi_b, in_=ATp), [B["PS"]], [B["V"]])
                dve(lambda: nc.vector.tensor_copy(out=hi_f, in_=hi_b), [B["V"]], [B["V"]])
                dve(lambda: nc.vector.tensor_tensor(out=hi_f, in0=ATp, in1=hi_f, op=ALU.subtract), [B["PS"], B["V"]], [B["V"]])
                dve(lambda: nc.vector.tensor_scalar(out=lo_b, in0=hi_f, scalar1=ml, scalar2=None, op0=ALU.mult),
                    [B["V"], cB], [B["V"]])
                dve(lambda: nc.vector.scalar_tensor_tensor(out=Vb, in0=hi_b, scalar=mh, in1=lo_b, op0=ALU.mult, op1=ALU.add),
                    [B["V"], cB], [B["V"]])
                Rv = v3(Rfl[:, 0:16 * T], T)
                dve(lambda: nc.vector.tensor_tensor(out=Rv[0:32], in0=Vb[0:32].unsqueeze(1).to_broadcast([32, 16, T]),
                                                    in1=RmB[0:32, :].unsqueeze(2).to_broadcast([32, 16, T]), op=ALU.mult),
                    [B["V"], cB], [B["Rfl"]])
                if rstate.get("T") != T:
                    rstate["T"] = T
                    dve(lambda: nc.vector.tensor_copy(out=Rv[32:64], in_=RmB[32:64, :].unsqueeze(2).to_broadcast([32, 16, T])),
                        [cB], [B["Rfl"]])
                dve(lambda: nc.vector.tensor_copy(out=lhsT_d[32:64, 0:T], in_=Vb[32:64]), [B["V"]], [B["lhsT_d"]])
                pe(lambda: nc.tensor.matmul(P_S[:, 16:32], lhsT=onesF[:T, :], rhs=sm[:T, 32:48], start=True, stop=True),
                   [B["A4"], cB], [B["PS"]])
                dve(lambda: nc.vector.tensor_tensor(out=sm[:T, 144:160], in0=P_S[:T, 16:32], in1=acum, op=ALU.subtract),
                    [B["PS"], B["A4"]], [B["dte"]])
                dve(lambda: nc.vector.tensor_copy(out=sm[:, 160:176], in_=P_S[:, 16:32]), [B["PS"]], [B["cd"]])
                act(lambda: nc.scalar.activation(out=sm[:T, 144:160], in_=sm[:T, 144:160], func=AF.Exp), [B["dte"]], [B["dte"]], tab="exp")
                act(lambda: nc.scalar.activation(out=sm[:, 160:176], in_=sm[:, 160:176], func=AF.Exp), [B["cd"]], [B["cd"]], tab="exp")
                act(lambda: nc.scalar.activation(out=sm[:T, 128:144], in_=acum, func=AF.Exp), [B["A4"]], [B["Ee"]], tab="exp")

            def d_cb():
                for g in range(2):
                    pe(lambda g=g: nc.tensor.matmul(P_S[:T, 256 + g * T:256 + (g + 1) * T], lhsT=actv[:, 8 + g, :],
                                                    rhs=actv[:, 10 + g, :], start=True, stop=True),
                       [actB], [B["PS"]], signal=(g == 1), n=T)
                dve(lambda: nc.vector.tensor_tensor(out=v3(CBm[:T, 0:2 * T], T), in0=v3(P_S[:T, 256:256 + 2 * T], T),
                                                    in1=triF[:T, :T].unsqueeze(1).to_broadcast([T, 2, T]), op=ALU.mult),
                    [B["PS"], cB], [B["CBm"]])
                for j in range(8):
                    pe(lambda j=j: nc.tensor.transpose(out=P_T[:T, j * 128:(j + 1) * 128], in_=actv[:, j, :], identity=identB[:]),
                       [actB, cB], [B["PT"]], signal=(j == 7), n=140)
                dve(lambda: nc.vector.tensor_tensor(out=v3(H4[:T, :], 64), in0=v3(P_T[:T, :], 64),
                                                    in1=dt_ap.unsqueeze(2).to_broadcast([T, 16, 64]), op=ALU.mult),
                    [B["PT"], dtB], [B["H4"]])
                for g in range(2):
                    pe(lambda g=g: nc.tensor.transpose(out=P_T[:T, g * 128:(g + 1) * 128], in_=actv[:, 8 + g, :],
                                                       identity=identB[:]), [actB, cB], [B["PT"]], signal=(g == 1), n=140)
                act(lambda: nc.scalar.copy(out=Btm[:T, :], in_=P_T[:T, 0:256]), [B["PT"]], [B["Btm"]], n=256)

            def mk_decay(g):
                def f():
                    Pd, PdB = (P_Y, B["PY"]) if g == 0 else (P_X, B["PX"])
                    ncol = 8 * T
                    negv = v3(NEG4[:T, 0:512], 128)[:, :, 0:T] if T != 128 else NEG4[:T, 0:512]
                    Mt, MB = Ms[g]
                    for hc in range(2):
                        o3 = v3(Pd[:T, hc * 4 * T:(hc + 1) * 4 * T], T) if T != 128 else Pd[:T, hc * 512:(hc + 1) * 512]
                        r0 = (g * 8 + hc * 4) * T
                        pe(lambda o3=o3, r0=r0: nc.tensor.matmul(o3, lhsT=lhsT_d[:, 0:T],
                                                                 rhs=(v3(Rfl[:, r0:r0 + 4 * T], T) if T != 128 else Rfl[:, r0:r0 + 512]),
                                                                 start=True, stop=False),
                           [B["lhsT_d"], B["Rfl"], cB], [PdB], signal=False)
                        pe(lambda o3=o3: nc.tensor.matmul(o3, lhsT=identB[:T, :T], rhs=negv, start=False, stop=True),
                           [cB], [PdB], signal=(hc == 1))
                    act(lambda: nc.scalar.activation(out=Mt[:T, 0:ncol], in_=Pd[:T, 0:ncol], func=AF.Exp), [PdB], [MB], tab="exp")
                    dve(lambda: nc.vector.tensor_tensor(out=v3(Mt[:T, 0:ncol], T), in0=v3(Mt[:T, 0:ncol], T),
                                                        in1=CBm[:T, g * T:(g + 1) * T].unsqueeze(1).to_broadcast([T, 8, T]),
                                                        op=ALU.mult), [MB, B["CBm"]], [MB], n=600)
                return f

            def mk_ydiag(b):
                def f():
                    Mt, MB = Ms[b]
                    for jj in range(4):
                        j = 4 * b + jj
                        pe(lambda j=j, jj=jj: nc.tensor.matmul(P_Y[:T, j * 128:(j + 1) * 128], lhsT=actv[:, j, :],
                                                               rhs=diagD[:, j, :], start=(jj == 0), stop=False,
                                                               skip_group_check=True),
                           [actB, cB], [B["PY"]], signal=False, n=200)
                    for hh in range(8):
                        h = 8 * b + hh
                        pe(lambda h=h, hh=hh: nc.tensor.matmul(P_Y[:T, h * 64:(h + 1) * 64], lhsT=Mt[:T, hh * T:(hh + 1) * T],
                                                               rhs=H4[:T, h * 64:(h + 1) * 64], start=False, stop=(hh == 7),
                                                               skip_group_check=True),
                           [MB, B["H4"]], [B["PY"]], signal=(hh == 7), n=200)
                return f

            def d_yoff():
                for g in range(2):
                    pe(lambda g=g: nc.tensor.matmul(P_X[:T, g * 512:(g + 1) * 512], lhsT=actv[:, 10 + g, :],
                                                    rhs=hTb[:, g * 512:(g + 1) * 512], start=True, stop=True),
                       [actB, B["hTb"]], [B["PX"]], signal=(g == 1))
                dve(lambda: nc.vector.tensor_tensor(out=v3(Ft[:T, :], 64), in0=v3(P_X[:T, :], 64),
                                                    in1=sm[:T, 128:144].unsqueeze(2).to_broadcast([T, 16, 64]), op=ALU.mult),
                    [B["PX"], B["Ee"]], [B["Ft"]])
                dve(lambda: nc.vector.tensor_tensor(out=Ft[:T, :], in0=P_Y[:T, :], in1=Ft[:T, :], op=ALU.add),
                    [B["PY"], B["Ft"]], [B["Ft"]])
                dve(lambda: nc.vector.tensor_tensor(out=Ft[:T, :], in0=Ft[:T, :], in1=F3t[:T, :], op=ALU.mult),
                    [B["Ft"], F3B], [B["Ft"]])
                act(lambda: nc.scalar.activation(out=H1[:T, :], in_=Ft[:T, :], func=AF.Square, accum_out=sm[:T, 10:11]),
                    [B["Ft"]], [B["H1"], B["ss2"]])
                rstd_from_ss(T, 10, B["ss2"], B["rs2"])
                dve(lambda: nc.vector.tensor_scalar(out=H1[:T, :], in0=Ft[:T, :], scalar1=sm[:T, 11:12], scalar2=None,
                                                    op0=ALU.mult), [B["Ft"], B["rs2"]], [B["H1"]], n=600)
                S.dma("ld_xr", Ft[:T, :], x_src, [], [B["Ft"]])
                for k in range(8):
                    pe(lambda k=k: nc.tensor.transpose(out=P_T[:, k * T:(k + 1) * T], in_=H1[:T, k * 128:(k + 1) * 128],
                                                       identity=identB[:T, :T]), [B["H1"], cB], [B["PT"]], signal=(k == 7), n=T)
                act(lambda: nc.scalar.copy(out=bT[:, 0:8 * T], in_=P_T[:, 0:8 * T]), [B["PT"]], [B["bT"]])

            def d_state():
                pool(lambda: nc.gpsimd.tensor_tensor(out=v3(H4[:T, :], 64), in0=v3(H4[:T, :], 64),
                                                     in1=sm[:T, 144:160].unsqueeze(2).to_broadcast([T, 16, 64]), op=ALU.mult),
                     [B["H4"], B["dte"]], [B["H4"]])
                for g in range(2):
                    pe(lambda g=g: nc.tensor.matmul(P_Y[:, g * 512:(g + 1) * 512], lhsT=Btm[:T, g * 128:(g + 1) * 128],
                                                    rhs=H4[:T, g * 512:(g + 1) * 512], start=True, stop=True),
                       [B["Btm"], B["H4"]], [B["PY"]], signal=(g == 1))
                pool(lambda: nc.gpsimd.tensor_tensor(out=v3(hT[:, :], 64), in0=v3(hT[:, :], 64),
                                                     in1=sm[:, 160:176].unsqueeze(2).to_broadcast([128, 16, 64]), op=ALU.mult),
                     [B["hT"], B["cd"]], [B["hT"]])
                dve(lambda: nc.vector.tensor_tensor(out=hT[:, :], in0=P_Y[:, :], in1=hT[:, :], op=ALU.add),
                    [B["PY"], B["hT"]], [B["hT"]])
                if not last:
                    act(lambda: nc.scalar.copy(out=hTb[:, :], in_=hT[:, :]), [B["hT"]], [B["hTb"]])

            def d_out():
                for part, (src, sB_) in enumerate(((aT, B["aT"]), (bT, B["bT"]))):
                    for cg in range(2):
                        for kk in range(8):
                            k = part * 8 + kk
                            pe(lambda cg=cg, k=k, src=src, kk=kk: nc.tensor.matmul(P_X[:T, cg * 512:(cg + 1) * 512],
                                                                                   lhsT=src[:, kk * T:(kk + 1) * T],
                                                                                   rhs=Wout[:, k, cg * 512:(cg + 1) * 512],
                                                                                   start=(k == 0), stop=(k == 15),
                                                                                   skip_group_check=True),
                               [sB_, B["WoA"], B["WoD"]], [B["PX"]], signal=(kk == 7 and cg == 1))
                dve(lambda: nc.vector.tensor_tensor(out=Ft[:T, :], in0=P_X[:T, :], in1=Ft[:T, :], op=ALU.add),
                    [B["PX"], B["Ft"]], [B["Ft"]])
                act(lambda: nc.scalar.activation(out=H1[:T, :], in_=Ft[:T, :], func=AF.Square, accum_out=sm[:T, 10:11]),
                    [B["Ft"]], [B["H1"], B["ss2"]])
                rstd_from_ss(T, 10, B["ss2"], B["rs2"])
                dve(lambda: nc.vector.scalar_tensor_tensor(out=Ft[:T, :], in0=Ft[:T, :], scalar=sm[:T, 11:12], in1=nf_bc[:T, :],
                                                           op0=ALU.mult, op1=ALU.mult), [B["Ft"], B["rs2"], cB], [B["Ft"]])
                S.dma("st_y", y_dst, Ft[:T, :], [B["Ft"]], [])
                if last:
                    for j in range(8):
                        pe(lambda j=j: nc.tensor.matmul(P_Y[:, j * 128:(j + 1) * 128], lhsT=hT[:, j * 128:(j + 1) * 128],
                                                        rhs=identF[:], start=True, stop=True),
                           [B["hT"], cB], [B["PY"]], signal=(j == 7), n=512)
                    st3o = ssm_dst.rearrange("(j p) n -> p j n", p=128)
                    for hf, (stg, stB) in enumerate(((aT, B["aT"]), (bT, B["bT"]))):
                        s32 = stg[:, :].bitcast(F32)
                        dve(lambda hf=hf, s32=s32: nc.vector.tensor_copy(out=s32[:, 0:512], in_=P_Y[:, hf * 512:(hf + 1) * 512]),
                            [B["PY"]], [stB], n=512)
                        S.dma("st_ssm%d" % hf, st3o[:, hf * 4:(hf + 1) * 4, :], s32[:, 0:512].rearrange("p (j n) -> p j n", n=128),
                              [stB], [])

            chunks += [d_acum, d_gmlp, d_cb, mk_decay(0), mk_decay(1), mk_ydiag(0), mk_ydiag(1), d_yoff, d_state, d_out]
            return chunks

        tiles = []
        for t in range(SEQ // 128):
            tiles.append(dict(T=128, x_src=xp[t * 128:(t + 1) * 128, :], y_dst=yp[t * 128:(t + 1) * 128, :],
                              first=(t == 0), last=(t == SEQ // 128 - 1), conv_dst=oconv_p, ssm_dst=ossm_p, v_dst=None,
                              init_ssm=None, init_conv=None))
        for b in range(NS):
            tiles.append(dict(T=TS, x_src=xs[b], y_dst=ys[b], first=True, last=True, conv_dst=oconv_s[b], ssm_dst=ossm_s[b],
                              v_dst=ov_s[b], init_ssm=sssm[b], init_conv=sconv[b]))

        def mk_load(idx):
            def f():
                if idx < len(tiles):
                    tt = tiles[idx]
                    S.dma("ld_x", XF[:tt["T"], :], tt["x_src"], [], [B["XF"]])
            return f

        mk_load(0)()
        for c in phase1(0, tiles[0], mk_load(1)):
            c()
        for i in range(len(tiles)):
            c2 = phase2(i, tiles[i])
            c1 = phase1(i + 1, tiles[i + 1], mk_load(i + 2)) if i + 1 < len(tiles) else []
            n1, n2 = len(c1), len(c2)
            a = b_ = 0
            while a < n2 or b_ < n1:
                if b_ >= n1 or (a < n2 and a * max(n1, 1) <= b_ * n2):
                    c2[a]()
                    a += 1
                else:
                    c1[b_]()
                    b_ += 1

        S.flush()
        for key in list(S.sem.keys()):
            if key.startswith("st_"):
                S._wait("sp", key, S.cnt[key])
    return nc


_NC_CACHE = {}


def kernel(x_prompt, x_sample, state_ssm, state_conv, norm_in_w, w_in, gmlp_ln_w, gmlp_ln_b, gmlp_ws, gmlp_bs,
           conv_w, conv_b, dt_bias, a_log, d_skip, ssm_norm_w, w_out, norm_f_w):
    f = lambda a: np.ascontiguousarray(np.asarray(a, dtype=np.float32))
    n = 8
    if "nc" not in _NC_CACHE:
        _NC_CACHE["nc"] = build_nc()
    nc = _NC_CACHE["nc"]
    shared = {
        "norm_in_w": f(norm_in_w[0]), "w_in": f(w_in[0]), "gmlp_ln_w": f(gmlp_ln_w[0]), "gmlp_ln_b": f(gmlp_ln_b[0]),
        "gmlp_ws": f(gmlp_ws[0]), "gmlp_bs": f(gmlp_bs[0]), "conv_w": f(conv_w[0]), "conv_b": f(conv_b[0]),
        "dt_bias": f(dt_bias[0]), "a_log": f(a_log[0]), "d_skip": f(d_skip[0]), "ssm_norm_w": f(ssm_norm_w[0]),
        "w_out": f(w_out[0]), "norm_f_w": f(norm_f_w),
    }
    in_maps = []
    for c in range(n):
        m = dict(shared)
        m["xp"] = f(x_prompt[c])
        m["xs"] = f(x_sample[NS * c:NS * (c + 1)])
        m["sssm"] = f(np.asarray(state_ssm)[0, NS * c:NS * (c + 1)].reshape(NS, 1024, 128))
        m["sconv"] = f(np.asarray(state_conv)[0, NS * c:NS * (c + 1)])
        in_maps.append(m)
    res = run_bass_kernel_spmd(nc, in_maps, core_ids=list(range(n)))
    R = res.results
    y_prompt = np.stack([R[c]["yp"] for c in range(n)], 0)
    y_sample = np.concatenate([R[c]["ys"] for c in range(n)], 0)
    ssm_p = np.stack([R[c]["ossm_p"].reshape(16, 64, 128) for c in range(n)], 0)[None]
    conv_p = np.stack([R[c]["oconv_p"] for c in range(n)], 0)[None]
    ssm_s = np.concatenate([R[c]["ossm_s"].reshape(NS, 16, 64, 128) for c in range(n)], 0)[None]
    conv_s = np.concatenate([R[c]["oconv_s"] for c in range(n)], 0)[None]
    v_s = np.concatenate([R[c]["ov_s"] for c in range(n)], 0)[None]
    return (y_prompt.astype(np.float32), y_sample.astype(np.float32), ssm_p.astype(np.float32), conv_p.astype(np.float32),
            ssm_s.astype(np.float32), conv_s.astype(np.float32), v_s.astype(np.float32))
```
